# Optimizing a Trainium2 kernel written in Bass

```python
import math
import jax, jax.numpy as jnp
from jax import lax
import numpy as np

D_MODEL = 2048
BATCH = 1
SEQ = 16384
DEPTH = 2

CHUNK = 64
Q_BLOCK = 128
HEAD_DIM = 128
N_HEADS = D_MODEL // HEAD_DIM
SB_HEADS = N_HEADS // 2
DIFF_HEADS = N_HEADS - SB_HEADS
SB_DIM = HEAD_DIM
DIFF_QK_DIM = HEAD_DIM // 2
DIFF_V_DIM = HEAD_DIM
ROT_DIM = DIFF_QK_DIM // 4
ROPE_THETA = 500000.0
SB_WIDTH = SB_HEADS * SB_DIM
DIFF_QK_WIDTH = DIFF_HEADS * 2 * DIFF_QK_DIM
DIFF_V_WIDTH = DIFF_HEADS * DIFF_V_DIM
MIX_WIDTH = SB_WIDTH + DIFF_V_WIDTH
IN_WIDTH = 3 * SB_WIDTH + 2 * DIFF_QK_WIDTH + DIFF_V_WIDTH
D_FF = (-(-8 * D_MODEL // 3) + 255) // 256 * 256
EPS = 1e-6

kernel_name = "stickbreak_diffattn_hybrid_encoder"


def lambda_init_value(layer):
    return 0.8 - 0.6 * math.exp(-0.3 * layer)


def rms_norm(x, g):
    xf = x.astype(jnp.float32)
    y = xf * lax.rsqrt(jnp.mean(xf * xf, axis=-1, keepdims=True) + EPS)
    return (y * g.astype(jnp.float32)).astype(x.dtype)


def partial_rope(x, pos):
    half = ROT_DIM // 2
    inv_freq = ROPE_THETA ** (-jnp.arange(0, ROT_DIM, 2, dtype=jnp.float32) / ROT_DIM)
    ang = pos.astype(jnp.float32)[:, None] * inv_freq[None, :]
    cos = jnp.cos(ang)[None, :, None, :]
    sin = jnp.sin(ang)[None, :, None, :]
    xr = x[..., :ROT_DIM].astype(jnp.float32)
    x1, x2 = xr[..., :half], xr[..., half:]
    rot = jnp.concatenate([x1 * cos - x2 * sin, x2 * cos + x1 * sin], axis=-1)
    return jnp.concatenate([rot.astype(x.dtype), x[..., ROT_DIM:]], axis=-1)


def to_blocks(t):
    b, s, h, d = t.shape
    return t.reshape(b, s // Q_BLOCK, Q_BLOCK, h, d).transpose(1, 0, 3, 2, 4)


def from_blocks(o):
    nb, b, h, qb, d = o.shape
    return o.transpose(1, 0, 3, 2, 4).reshape(b, nb * qb, h, d)


def stick_breaking_attention(q, k, v):
    s_len = q.shape[1]
    scale = 1.0 / math.sqrt(q.shape[-1])
    k_t = k.transpose(0, 2, 1, 3)
    v_t = v.transpose(0, 2, 1, 3)
    kpos = jnp.arange(s_len)

    def block(args):
        qb, i = args
        qpos = i * Q_BLOCK + jnp.arange(Q_BLOCK)
        z = jnp.einsum('bhqd,bhkd->bhqk', qb, k_t).astype(jnp.float32) * scale
        causal = kpos[None, :] < qpos[:, None]
        log_keep = jnp.where(causal, jax.nn.log_sigmoid(-z), 0.0)
        between = lax.cumsum(log_keep, axis=3, reverse=True) - log_keep
        w = jnp.where(causal, jnp.exp(jax.nn.log_sigmoid(z) + between), 0.0)
        return jnp.einsum('bhqk,bhkd->bhqd', w.astype(v.dtype), v_t)

    out = lax.map(block, (to_blocks(q), jnp.arange(s_len // Q_BLOCK)))
    return from_blocks(out)


def differential_attention(q1, q2, k1, k2, v, lam):
    s_len = q1.shape[1]
    scale = 1.0 / math.sqrt(q1.shape[-1])
    k1_t = k1.transpose(0, 2, 1, 3)
    k2_t = k2.transpose(0, 2, 1, 3)
    v_t = v.transpose(0, 2, 1, 3)
    kchunk = jnp.arange(s_len) // CHUNK

    def block(args):
        q1b, q2b, i = args
        qchunk = (i * Q_BLOCK + jnp.arange(Q_BLOCK)) // CHUNK
        allowed = kchunk[None, :] <= qchunk[:, None]

        def probs(qb, kt):
            sc = jnp.einsum('bhqd,bhkd->bhqk', qb, kt).astype(jnp.float32) * scale
            return jax.nn.softmax(jnp.where(allowed, sc, -jnp.inf), axis=-1)

        p = probs(q1b, k1_t) - lam * probs(q2b, k2_t)
        return jnp.einsum('bhqk,bhkd->bhqd', p.astype(v.dtype), v_t)

    out = lax.map(block, (to_blocks(q1), to_blocks(q2), jnp.arange(s_len // Q_BLOCK)))
    return from_blocks(out)


def hybrid_mixer(h, layer, w_in, w_o, sb_out_norm, diff_subln, lq1, lk1, lq2, lk2):
    b, s, _ = h.shape
    pos = jnp.arange(s)
    proj = h @ w_in
    offs = np.cumsum([SB_WIDTH, SB_WIDTH, SB_WIDTH, DIFF_QK_WIDTH, DIFF_QK_WIDTH])
    q_sb, k_sb, v_sb, q_d, k_d, v_d = jnp.split(proj, offs.tolist(), axis=-1)

    q_sb = q_sb.reshape(b, s, SB_HEADS, SB_DIM)
    k_sb = k_sb.reshape(b, s, SB_HEADS, SB_DIM)
    v_sb = v_sb.reshape(b, s, SB_HEADS, SB_DIM)
    o_sb = stick_breaking_attention(q_sb, k_sb, v_sb)
    o_sb = rms_norm(o_sb, sb_out_norm)

    q_d = partial_rope(q_d.reshape(b, s, DIFF_HEADS * 2, DIFF_QK_DIM), pos)
    k_d = partial_rope(k_d.reshape(b, s, DIFF_HEADS * 2, DIFF_QK_DIM), pos)
    q1, q2 = q_d[:, :, 0::2], q_d[:, :, 1::2]
    k1, k2 = k_d[:, :, 0::2], k_d[:, :, 1::2]
    v_d = v_d.reshape(b, s, DIFF_HEADS, DIFF_V_DIM)
    lam_init = lambda_init_value(layer)
    lam = (jnp.exp(jnp.sum(lq1.astype(jnp.float32) * lk1.astype(jnp.float32)))
           - jnp.exp(jnp.sum(lq2.astype(jnp.float32) * lk2.astype(jnp.float32)))
           + lam_init)
    o_d = differential_attention(q1, q2, k1, k2, v_d, lam)
    o_d = rms_norm(o_d, diff_subln) * (1.0 - lam_init)

    mix = jnp.concatenate([o_sb.reshape(b, s, SB_WIDTH),
                           o_d.reshape(b, s, DIFF_V_WIDTH)], axis=-1)
    return mix @ w_o


def swiglu(h, w_gate, w_up, w_down):
    return (jax.nn.silu(h @ w_gate) * (h @ w_up)) @ w_down


def setup_inputs(seed: int = 0) -> dict:
    key = jax.random.key(seed)
    ks = jax.random.split(key, 16)
    f32 = jnp.float32

    def nrm(k, shape, fan_in):
        return jax.random.normal(k, shape, f32) * (fan_in ** -0.5)

    def gain(k, shape):
        return 1.0 + 0.02 * jax.random.normal(k, shape, f32)

    return {
        "x": jax.random.normal(ks[0], (BATCH, SEQ, D_MODEL), f32),
        "w_in": nrm(ks[1], (DEPTH, D_MODEL, IN_WIDTH), D_MODEL),
        "w_o": nrm(ks[2], (DEPTH, MIX_WIDTH, D_MODEL), MIX_WIDTH),
        "sb_out_norm": gain(ks[3], (DEPTH, SB_HEADS, SB_DIM)),
        "diff_subln": gain(ks[4], (DEPTH, DIFF_HEADS, DIFF_V_DIM)),
        "lambda_q1": 0.1 * jax.random.normal(ks[5], (DEPTH, DIFF_QK_DIM), f32),
        "lambda_k1": 0.1 * jax.random.normal(ks[6], (DEPTH, DIFF_QK_DIM), f32),
        "lambda_q2": 0.1 * jax.random.normal(ks[7], (DEPTH, DIFF_QK_DIM), f32),
        "lambda_k2": 0.1 * jax.random.normal(ks[8], (DEPTH, DIFF_QK_DIM), f32),
        "pre_mix_norm": gain(ks[9], (DEPTH, D_MODEL)),
        "post_mix_norm": gain(ks[10], (DEPTH, D_MODEL)),
        "pre_ffn_norm": gain(ks[11], (DEPTH, D_MODEL)),
        "post_ffn_norm": gain(ks[12], (DEPTH, D_MODEL)),
        "w_gate": nrm(ks[13], (DEPTH, D_MODEL, D_FF), D_MODEL),
        "w_up": nrm(ks[14], (DEPTH, D_MODEL, D_FF), D_MODEL),
        "w_down": nrm(ks[15], (DEPTH, D_FF, D_MODEL), D_FF),
    }


def reference(x, w_in, w_o, sb_out_norm, diff_subln, lambda_q1, lambda_k1,
              lambda_q2, lambda_k2, pre_mix_norm, post_mix_norm, pre_ffn_norm,
              post_ffn_norm, w_gate, w_up, w_down):
    for l in range(DEPTH):
        h = rms_norm(x, pre_mix_norm[l])
        m = hybrid_mixer(h, l, w_in[l], w_o[l], sb_out_norm[l], diff_subln[l],
                         lambda_q1[l], lambda_k1[l], lambda_q2[l], lambda_k2[l])
        x = x + rms_norm(m, post_mix_norm[l])
        h = rms_norm(x, pre_ffn_norm[l])
        f = swiglu(h, w_gate[l], w_up[l], w_down[l])
        x = x + rms_norm(f, post_ffn_norm[l])
    return x
```

```python
import math
import numpy as np
import ml_dtypes
from contextlib import ExitStack
import concourse.bass as bass
import concourse.mybir as mybir
from concourse.bass_utils import run_bass_kernel_spmd

F32 = mybir.dt.float32
BF16 = mybir.dt.bfloat16
AF = mybir.ActivationFunctionType
ALU = mybir.AluOpType
AX = mybir.AxisListType
EPS = 1e-6
SAME_ENGINE_SYNC = True
_SKIP = set()


class Res:
    __slots__ = ("w", "r", "name")

    def __init__(self, name=""):
        self.w = {}
        self.r = {}
        self.name = name


class FW:
    def __init__(self, nc, n_dma_sems=6):
        self.nc = nc
        self.es = ExitStack()
        self.eng = {"pe": nc.tensor, "act": nc.scalar, "dve": nc.vector,
                    "pool": nc.gpsimd, "sp": nc.sync}
        self.sems = {}
        self.cnt = {}
        for k in self.eng:
            self._mksem(k)
        self.known = {k: {} for k in self.eng}
        self.dq = {}
        for q in ("sp", "pool", "act"):
            keys = []
            for j in range(n_dma_sems):
                key = "d_%s%d" % (q, j)
                self._mksem(key)
                keys.append(key)
            self.dq[q] = {"keys": keys, "i": 0}
        self.n_inst = 0

    def _mksem(self, key):
        self.sems[key] = self.es.enter_context(self.nc.semaphore("s_" + key))
        self.cnt[key] = 0

    def _wait(self, e, deps):
        for k, v in deps.items():
            if v <= 0 or self.known[e].get(k, 0) >= v:
                continue
            if k == e and (e == "pe" or not SAME_ENGINE_SYNC):
                continue
            self.eng[e].wait_ge(self.sems[k], v)
            self.known[e][k] = v

    @staticmethod
    def _add(deps, d):
        for k, v in d.items():
            if deps.get(k, 0) < v:
                deps[k] = v

    def _deps(self, reads, writes, partial):
        deps = {}
        for r in reads:
            self._add(deps, r.w)
        for w in writes:
            self._add(deps, w.r)
            if not partial:
                self._add(deps, w.w)
        return deps

    def _commit(self, key, val, reads, writes, partial):
        for r in reads:
            if r.r.get(key, 0) < val:
                r.r[key] = val
        for w in writes:
            if partial:
                if w.w.get(key, 0) < val:
                    w.w[key] = val
            else:
                w.w = {key: val}
            w.r = {}

    def op(self, e, fn, reads=(), writes=(), partial=False):
        deps = self._deps(reads, writes, partial)
        self._wait(e, deps)
        inst = fn(self.eng[e])
        self.cnt[e] += 1
        inst.then_inc(self.sems[e], 1)
        self._commit(e, self.cnt[e], reads, writes, partial)
        self.n_inst += 1
        return inst

    def dma(self, q, out, in_, reads=(), writes=(), partial=False, **kw):
        deps = self._deps(reads, writes, partial)
        pool = self.dq[q]
        key = pool["keys"][pool["i"] % len(pool["keys"])]
        pool["i"] += 1
        if deps.get(key, 0) < self.cnt[key]:
            deps[key] = self.cnt[key]
        self._wait(q, deps)
        inst = self.eng[q].dma_start(out=out, in_=in_, **kw)
        self.cnt[key] += 16
        inst.then_inc(self.sems[key], 16)
        self._commit(key, self.cnt[key], reads, writes, partial)
        self.n_inst += 1
        return inst

    def barrier(self, engines=None):
        for e in (engines or self.eng):
            self._wait(e, dict(self.cnt))

    def finish(self):
        self._wait("sp", dict(self.cnt))
        self.es.close()


def sb(es, nc, name, shape, dtype):
    return es.enter_context(nc.sbuf_tensor(name, list(shape), dtype))


def build_A(S, D=2048, out_name="mixT"):
    nc = bass.Bass("TRN2", target_bir_lowering=False)
    KD = D // 128
    NG = S // 512
    NBLK = S // 128
    dt = nc.dram_tensor
    x = dt("x", [S, D], F32, kind="ExternalInput").ap()
    wa = dt("wa", [D, 768], F32, kind="ExternalInput").ap()
    permT = dt("permT", [128, 128], F32, kind="ExternalInput").ap()
    gpre = dt("gpre", [D], F32, kind="ExternalInput").ap()
    hg = dt("hg", [128, 2], F32, kind="ExternalInput").ap()
    lamv = dt("lamv", [4, 64], F32, kind="ExternalInput").ap()
    consts = dt("consts", [128, 2], F32, kind="ExternalInput").ap()
    cos_t = dt("cos_t", [128, S], F32, kind="ExternalInput").ap()
    sin_t = dt("sin_t", [128, S], F32, kind="ExternalInput").ap()
    mixT = dt(out_name, [256, S], BF16, kind="ExternalOutput").ap()
    qsb_s = dt("qsb_s", [128, S], BF16, kind="Internal").ap()
    ksb_s = dt("ksb_s", [128, S], BF16, kind="Internal").ap()
    vsb_s = dt("vsb_s", [128, NBLK, 128], BF16, kind="Internal").ap()
    qd_s = dt("qd_s", [128, S], BF16, kind="Internal").ap()
    kd_s = dt("kd_s", [128, S], BF16, kind="Internal").ap()
    vd_s = dt("vd_s", [128, NBLK, 128], BF16, kind="Internal").ap()

    fw = FW(nc)
    es = fw.es
    pp = [es.enter_context(nc.psum_tensor("pp%d" % i, [128, 2, 512], F32)) for i in range(3)]
    ps = [pp[i // 2][:, i % 2, :] for i in range(6)]
    ps += [es.enter_context(nc.psum_tensor("ps%d" % i, [128, 512], F32)) for i in range(6, 8)]
    psr = [Res("ps%d" % i) for i in range(8)]

    ident = sb(es, nc, "ident", [128, 128], BF16)
    onesb = sb(es, nc, "onesb", [128, 128], BF16)
    onesneg = sb(es, nc, "onesneg", [128, 128], BF16)
    uneg = sb(es, nc, "uneg", [128, 128], BF16)
    mtri = sb(es, nc, "mtri", [128, 128], BF16)
    onesf = sb(es, nc, "onesf", [128, 128], F32)
    r_c = Res()
    fw.op("pool", lambda e: e.memset(onesb[:], 1.0), writes=[r_c])
    fw.op("pool", lambda e: e.memset(onesneg[:], -1.0), writes=[r_c])
    fw.op("pool", lambda e: e.memset(onesf[:], 1.0), writes=[r_c])
    fw.op("pool", lambda e: e.memset(ident[:], 0.0), writes=[r_c])
    fw.op("pool", lambda e: e.memset(uneg[:], 0.0), writes=[r_c])
    fw.op("pool", lambda e: e.memset(mtri[:], 0.0), writes=[r_c])
    fw.op("pool", lambda e: e.affine_select(out=ident[:], in_=onesb[:], pattern=[[-1, 128]], base=0,
                                            channel_multiplier=1, compare_op=ALU.is_equal, fill=0.0), writes=[r_c])
    fw.op("pool", lambda e: e.affine_select(out=uneg[:], in_=onesneg[:], pattern=[[-1, 128]], base=0,
                                            channel_multiplier=1, compare_op=ALU.is_ge, fill=0.0), writes=[r_c])
    fw.op("pool", lambda e: e.affine_select(out=mtri[:], in_=onesb[:], pattern=[[1, 128]], base=0,
                                            channel_multiplier=-1, compare_op=ALU.is_gt, fill=0.0), writes=[r_c])
    hgt = sb(es, nc, "hgt", [128, 2], F32)
    cst = sb(es, nc, "cst", [128, 2], F32)
    lamt = sb(es, nc, "lamt", [128, 4, 64], F32)
    lsc = sb(es, nc, "lsc", [128, 8], F32)
    ltmp = sb(es, nc, "ltmp", [128, 64], F32)
    r_l = Res()
    fw.dma("sp", hgt[:], hg, writes=[r_l], partial=True)
    fw.dma("sp", cst[:], consts, writes=[r_l], partial=True)
    for i in range(4):
        fw.dma("sp", lamt[:, i, :], lamv[i, :].partition_broadcast(128), writes=[r_l], partial=True)
    for i in range(2):
        fw.op("dve", lambda e, i=i: e.tensor_tensor(out=ltmp[:], in0=lamt[:, 2 * i, :], in1=lamt[:, 2 * i + 1, :],
                                                    op=ALU.mult), reads=[r_l], writes=[r_l])
        fw.op("dve", lambda e, i=i: e.tensor_reduce(out=lsc[:, i:i + 1], in_=ltmp[:], axis=AX.X, op=ALU.add),
              reads=[r_l], writes=[r_l])
        fw.op("act", lambda e, i=i: e.activation(out=lsc[:, 2 + i:3 + i], in_=lsc[:, i:i + 1], func=AF.Exp),
              reads=[r_l], writes=[r_l])
    fw.op("dve", lambda e: e.tensor_tensor(out=lsc[:, 6:7], in0=lsc[:, 3:4], in1=lsc[:, 2:3], op=ALU.subtract),
          reads=[r_l], writes=[r_l])
    fw.op("dve", lambda e: e.tensor_tensor(out=lsc[:, 4:5], in0=lsc[:, 6:7], in1=cst[:, 0:1], op=ALU.subtract),
          reads=[r_l], writes=[r_l])
    fw.op("dve", lambda e: e.tensor_tensor(out=lsc[:, 5:6], in0=hgt[:, 1:2], in1=cst[:, 1:2], op=ALU.mult),
          reads=[r_l], writes=[r_l])
    neglam = lsc[:, 4:5]
    gdiff = lsc[:, 5:6]
    gsb = hgt[:, 0:1]

    with ExitStack() as e1:
        G = sb(e1, nc, "G", [128, D], F32)
        r_G = Res()
        fw.dma("sp", G[:], gpre.partition_broadcast(128), writes=[r_G])
        wb = sb(e1, nc, "wb", [128, KD, 768], BF16)
        pmT = sb(e1, nc, "pmT", [128, 128], BF16)
        r_pm = Res()
        fw.dma("pool", pmT[:], permT, writes=[r_pm])
        qhl = [sb(e1, nc, "qhl%d" % i, [128, 2, 512], BF16) for i in range(2)]
        r_qhl = [Res() for _ in range(2)]
        qres = [sb(e1, nc, "qres%d" % i, [128, 512], F32) for i in range(2)]
        r_qres = [Res() for _ in range(2)]
        r_w = Res()
        wa_v = wa.rearrange("(k p) c -> p k c", p=128)
        for k in range(0, KD, 2):
            fw.dma("pool", wb[:, k:k + 2, :], wa_v[:, k:k + 2, :], writes=[r_w], partial=True)
        NXB = 3
        xt = [sb(e1, nc, "xt%d" % i, [128, D], F32) for i in range(NXB)]
        r_xt = [Res() for _ in range(NXB)]
        hb = [sb(e1, nc, "hb%d" % i, [128, D], BF16) for i in range(2)]
        r_hb = [Res() for _ in range(2)]
        junk = sb(e1, nc, "junk", [128, D], BF16)
        r_junk = Res()
        st = [sb(e1, nc, "st%d" % i, [128, 4], F32) for i in range(2)]
        r_st = [Res() for _ in range(2)]
        hT = [sb(e1, nc, "hT%d" % i, [128, KD, 512], BF16) for i in range(2)]
        r_hT = [[Res() for _ in range(4)] for _ in range(2)]
        cs = [sb(e1, nc, "cs%d" % i, [128, 2, 512], F32) for i in range(2)]
        r_cs = [Res() for _ in range(2)]
        qk_st = [sb(e1, nc, "qkst%d" % i, [128, 4, 512], BF16) for i in range(2)]
        r_qk = [[Res() for _ in range(4)] for _ in range(2)]
        v_st = [sb(e1, nc, "vst%d" % i, [128, 256], BF16) for i in range(2)]
        r_v = [Res() for _ in range(2)]
        t12 = [sb(e1, nc, "t12%d" % i, [128, 2, 512], F32) for i in range(2)]
        r_t12 = [Res() for _ in range(2)]

        def loadx(tt):
            b = tt % NXB
            fw.dma("sp", xt[b][:], x[tt * 128:(tt + 1) * 128, :], writes=[r_xt[b]])

        proj_banks = [0, 1, 2, 3, 4, 5]
        pbi = [0]

        def nextbank():
            bnk = proj_banks[pbi[0] % len(proj_banks)]
            pbi[0] += 1
            return bnk

        loadx(0)
        loadx(1)
        vcnt = 0
        t12c = 0

        def chain(g, t):
            gb = g % 2
            if t == 0:
                fw.dma("sp", cs[gb][:, 0, :], cos_t[:, g * 512:(g + 1) * 512], writes=[r_cs[gb]])
                fw.dma("sp", cs[gb][:, 1, :], sin_t[:, g * 512:(g + 1) * 512], writes=[r_cs[gb]], partial=True)
            tt = g * 4 + t
            b = tt % NXB
            b2 = tt % 2
            if tt + 2 < NBLK:
                loadx(tt + 2)
            fw.op("act", lambda e: e.activation(out=junk[:], in_=xt[b][:], func=AF.Square,
                                                accum_out=st[b2][:, 0:1]),
                  reads=[r_xt[b]], writes=[r_junk, r_st[b2]])
            fw.op("dve", lambda e: e.tensor_scalar(out=st[b2][:, 1:2], in0=st[b2][:, 0:1], scalar1=1.0 / D,
                                                   scalar2=EPS, op0=ALU.mult, op1=ALU.add),
                  reads=[r_st[b2]], writes=[r_st[b2]])
            fw.op("act", lambda e: e.activation(out=st[b2][:, 2:3], in_=st[b2][:, 1:2], func=AF.Sqrt),
                  reads=[r_st[b2]], writes=[r_st[b2]])
            fw.op("dve", lambda e: e.reciprocal(out=st[b2][:, 3:4], in_=st[b2][:, 2:3]),
                  reads=[r_st[b2]], writes=[r_st[b2]])
            fw.op("dve", lambda e: e.scalar_tensor_tensor(
                out=hb[b2][:], in0=xt[b][:], scalar=st[b2][:, 3:4], in1=G[:], op0=ALU.mult, op1=ALU.mult),
                reads=[r_xt[b], r_st[b2], r_G], writes=[r_hb[b2]])

        def trans(g, t):
            gb = g % 2
            b2 = (g * 4 + t) % 2
            for h in range(2):
                pv = ps[6 + h][:].bitcast(BF16)
                for kk in range(8):
                    k = h * 8 + kk
                    fw.op("pe", lambda e, k=k, kk=kk, pv=pv: e.transpose(
                        pv[:, kk * 128:(kk + 1) * 128], hb[b2][:, k * 128:(k + 1) * 128], ident[:]),
                        reads=[r_hb[b2], r_c], writes=[psr[6 + h]], partial=(kk > 0))
                if h == 0:
                    fw.op("dve", lambda e, pv=pv: e.tensor_copy(
                        out=hT[gb][:, 0:8, t * 128:(t + 1) * 128], in_=pv.rearrange("p (k t) -> p k t", k=8)),
                        reads=[psr[6]], writes=[r_hT[gb][t]])
                else:
                    fw.op("act", lambda e, pv=pv: e.activation(
                        out=hT[gb][:, 8:16, t * 128:(t + 1) * 128], in_=pv.rearrange("p (k t) -> p k t", k=8),
                        func=AF.Copy),
                        reads=[psr[7]], writes=[r_hT[gb][t]], partial=True)

        def projs(g, hooks):
            nonlocal vcnt, t12c
            gb = g % 2

            def hook(u):
                for fn in hooks.get(u, ()):
                    fn()

            def proj(col0):
                bnk = nextbank()
                for k in range(KD):
                    fw.op("pe", lambda e, k=k: e.matmul(
                        ps[bnk][:], lhsT=wb[:, k, col0:col0 + 128], rhs=hT[gb][:, k, :],
                        start=(k == 0), stop=(k == KD - 1)),
                        reads=[r_w] + r_hT[gb], writes=[psr[bnk]])
                return bnk

            hook(-1)
            bq = proj(0)
            fw.op("act", lambda e: e.activation(out=qk_st[gb][:, 0, :], in_=ps[bq][:], func=AF.Copy,
                                                scale=1.0 / math.sqrt(128.0)),
                  reads=[psr[bq]], writes=[r_qk[gb][0]])
            fw.dma("pool", qsb_s[:, g * 512:(g + 1) * 512], qk_st[gb][:, 0, :], reads=[r_qk[gb][0]])
            hook(0)
            bk = proj(128)
            fw.op("dve", lambda e: e.tensor_copy(out=qk_st[gb][:, 1, :], in_=ps[bk][:]),
                  reads=[psr[bk]], writes=[r_qk[gb][1]])
            fw.dma("pool", ksb_s[:, g * 512:(g + 1) * 512], qk_st[gb][:, 1, :], reads=[r_qk[gb][1]])
            hook(1)
            late = []
            for idx, (c_a, scale, dst) in enumerate(((256, 0.125, qd_s), (384, 1.0, kd_s))):
                ba = proj(c_a)
                tb_ = t12c % 2
                t12c += 1
                fw.op("act", lambda e: e.activation(out=qhl[tb_][:, 0, :], in_=ps[ba][:], func=AF.Copy),
                      reads=[psr[ba]], writes=[r_qhl[tb_]])
                fw.op("pool", lambda e: e.tensor_copy(out=qres[tb_][:], in_=qhl[tb_][:, 0, :]),
                      reads=[r_qhl[tb_]], writes=[r_qres[tb_]])
                fw.op("dve", lambda e: e.tensor_tensor(out=qhl[tb_][:, 1, :], in0=ps[ba][:], in1=qres[tb_][:],
                                                       op=ALU.subtract),
                      reads=[psr[ba], r_qres[tb_]], writes=[r_qhl[tb_]], partial=True)
                fw.op("dve", lambda e: e.scalar_tensor_tensor(
                    out=t12[tb_][:, 0, :], in0=ps[ba][:], scalar=scale, in1=cs[gb][:, 0, :],
                    op0=ALU.mult, op1=ALU.mult),
                    reads=[psr[ba], r_cs[gb]], writes=[r_t12[tb_]])
                late.append((idx, tb_, scale, dst))
                hook(2 + idx)
            for t in range(4):
                tt = g * 4 + t
                bnk = nextbank()
                for k in range(KD):
                    fw.op("pe", lambda e, k=k: e.matmul(
                        ps[bnk][:, 0:256], lhsT=hT[gb][:, k, t * 128:(t + 1) * 128], rhs=wb[:, k, 512:768],
                        start=(k == 0), stop=(k == KD - 1)),
                        reads=[r_w, r_hT[gb][t]], writes=[psr[bnk]])
                vb = vcnt % 2
                vcnt += 1
                fw.op("dve" if t % 2 == 0 else "act",
                      (lambda e: e.tensor_copy(out=v_st[vb][:], in_=ps[bnk][:, 0:256])) if t % 2 == 0 else
                      (lambda e: e.activation(out=v_st[vb][:], in_=ps[bnk][:, 0:256], func=AF.Copy)),
                      reads=[psr[bnk]], writes=[r_v[vb]])
                fw.dma("pool", vsb_s[:, tt, :], v_st[vb][:, 0:128], reads=[r_v[vb]])
                fw.dma("pool", vd_s[:, tt, :], v_st[vb][:, 128:256], reads=[r_v[vb]])
                hook(4 + t)
            for idx, tb_, scale, dst in late:
                bp = nextbank()
                for j in range(2):
                    fw.op("pe", lambda e, j=j: e.matmul(ps[bp][:], lhsT=pmT[:], rhs=qhl[tb_][:, j, :],
                                                        start=(j == 0), stop=(j == 1)),
                          reads=[r_pm, r_qhl[tb_]], writes=[psr[bp]])
                fw.op("dve", lambda e: e.scalar_tensor_tensor(
                    out=t12[tb_][:, 1, :], in0=ps[bp][:], scalar=scale, in1=cs[gb][:, 1, :],
                    op0=ALU.mult, op1=ALU.mult),
                    reads=[psr[bp], r_cs[gb]], writes=[r_t12[tb_]], partial=True)
                fw.op("pool", lambda e: e.tensor_tensor(
                    out=qk_st[gb][:, 2 + idx, :], in0=t12[tb_][:, 0, :], in1=t12[tb_][:, 1, :], op=ALU.add),
                    reads=[r_t12[tb_]], writes=[r_qk[gb][2 + idx]])
                fw.dma("pool", dst[:, g * 512:(g + 1) * 512], qk_st[gb][:, 2 + idx, :], reads=[r_qk[gb][2 + idx]])

        for t in range(4):
            chain(0, t)
            trans(0, t)
        for g in range(NG):
            hooks = {}
            if g + 1 < NG:
                n = g + 1
                hooks = {-1: [lambda: chain(n, 0)],
                         0: [lambda: trans(n, 0), lambda: chain(n, 1)],
                         2: [lambda: trans(n, 1), lambda: chain(n, 2)],
                         3: [lambda: trans(n, 2), lambda: chain(n, 3)],
                         5: [lambda: trans(n, 3)]}
            projs(g, hooks)
        fw.barrier()

    NQ = S // 512
    blocks = []
    for qi in range(NQ):
        for kb in range(4 * qi + 3, -1, -1):
            o = kb * 128 - qi * 512
            blocks.append(dict(qi=qi, kb=kb, o=o, diag=(o >= 0), c0=max(o, 0), first=(kb == 4 * qi + 3),
                               last=(kb == 0), idx=len(blocks)))
    NBK = len(blocks)

    with ExitStack() as e2:
        KT = sb(e2, nc, "KT", [128, S], BF16)
        V = sb(e2, nc, "V", [128, NBLK, 128], BF16)
        KTd = sb(e2, nc, "KTd", [128, S], BF16)
        Vd = sb(e2, nc, "Vd", [128, NBLK, 128], BF16)
        CH = min(2048, S)
        VB = CH // 128
        NCH = S // CH
        r_KTc = [Res() for _ in range(NCH)]
        r_Vc = [Res() for _ in range(NCH)]
        r_KTdc = [Res() for _ in range(NCH)]
        r_Vdc = [Res() for _ in range(NCH)]
        for ci in range(NCH):
            c0, b0 = ci * CH, ci * VB
            fw.dma("sp", KT[:, c0:c0 + CH], ksb_s[:, c0:c0 + CH], writes=[r_KTc[ci]])
            fw.dma("sp", V[:, b0:b0 + VB, :], vsb_s[:, b0:b0 + VB, :], writes=[r_Vc[ci]])
            fw.dma("sp", KTd[:, c0:c0 + CH], kd_s[:, c0:c0 + CH], writes=[r_KTdc[ci]])
            fw.dma("sp", Vd[:, b0:b0 + VB, :], vd_s[:, b0:b0 + VB, :], writes=[r_Vdc[ci]])
        Q = [sb(e2, nc, "Q%d" % i, [128, 512], BF16) for i in range(2)]
        r_Q = [Res() for _ in range(2)]
        Qd = [sb(e2, nc, "Qd%d" % i, [128, 512], BF16) for i in range(2)]
        r_Qd = [Res() for _ in range(2)]
        NE, NL, NW = 2, 3, 3
        E = [sb(e2, nc, "E%d" % i, [128, 512], F32) for i in range(NE)]
        r_E = [Res() for _ in range(NE)]
        L = [sb(e2, nc, "L%d" % i, [128, 512], BF16) for i in range(NL)]
        r_L = [Res() for _ in range(NL)]
        R = [sb(e2, nc, "R%d" % i, [128, 512], BF16) for i in range(2)]
        r_R = [Res() for _ in range(2)]
        W = [sb(e2, nc, "W%d" % i, [128, 512], BF16) for i in range(NW)]
        r_W = [Res() for _ in range(NW)]
        Wf = [sb(e2, nc, "Wf%d" % i, [128, 512], BF16) for i in range(2)]
        r_Wf = [Res() for _ in range(2)]
        o_sb = sb(e2, nc, "o_sb", [128, 512], F32)
        r_osbf = Res()
        sq = sb(e2, nc, "sq", [128, 512], F32)
        r_sq = Res()
        nv = sb(e2, nc, "nv", [128, 2, 512], F32)
        r_nv = Res()
        osb = [sb(e2, nc, "osb%d" % i, [128, 512], BF16) for i in range(2)]
        r_osb = [Res() for _ in range(2)]
        NP = 3
        P = [sb(e2, nc, "P%d" % i, [128, 2, 512], BF16) for i in range(NP)]
        r_P = [Res() for _ in range(NP)]
        Pf = [sb(e2, nc, "Pf%d" % i, [128, 2, 512], BF16) for i in range(2)]
        r_Pf = [Res() for _ in range(2)]
        PS = [sb(e2, nc, "PS%d" % i, [128, 2, 512], F32) for i in range(2)]
        r_PS = [[Res() for _ in range(2)] for _ in range(2)]
        acc = sb(e2, nc, "acc", [128, 2, 512], F32)
        r_acc = [Res() for _ in range(2)]
        rc = sb(e2, nc, "rc", [128, 2, 512], F32)
        r_rc = [Res() for _ in range(2)]
        ab = sb(e2, nc, "ab", [128, 2, 512], F32)
        r_ab = Res()
        od = sb(e2, nc, "od", [128, 512], F32)
        r_od = Res()
        sqd = sb(e2, nc, "sqd", [128, 512], F32)
        r_sqd = Res()
        nvd = sb(e2, nc, "nvd", [128, 2, 512], F32)
        r_nvd = Res()
        osbd = [sb(e2, nc, "osbd%d" % i, [128, 512], BF16) for i in range(2)]
        r_osbd = [Res() for _ in range(2)]
        for i in range(2):
            fw.op("pool", lambda e, i=i: e.memset(Wf[i][:], 0.0), writes=[r_Wf[i]])
            fw.op("pool", lambda e, i=i: e.memset(Pf[i][:], 0.0), writes=[r_Pf[i]])
        ZB = [0, 1, 2]
        OB = 3
        ZD = 4
        OD = 6
        nbk = 0 if 'A2' in _SKIP else NBK
        if nbk:
            fw.dma("sp", Q[0][:], qsb_s[:, 0:512], writes=[r_Q[0]])
            fw.dma("sp", Qd[0][:], qd_s[:, 0:512], writes=[r_Qd[0]])
        ri = 0
        wi = 0
        pi = 0

        def s1(b):
            qi, kb, c0 = b["qi"], b["kb"], b["c0"]
            qb = qi % 2
            if b["first"] and qi + 1 < NQ:
                fw.dma("sp", Q[(qi + 1) % 2][:], qsb_s[:, (qi + 1) * 512:(qi + 2) * 512],
                       writes=[r_Q[(qi + 1) % 2]])
            zb = ZB[b["idx"] % 3]
            fw.op("pe", lambda e: e.matmul(ps[zb][:, c0:512], lhsT=KT[:, kb * 128:(kb + 1) * 128],
                                           rhs=Q[qb][:, c0:512], start=True, stop=True),
                  reads=[r_KTc[kb // VB], r_Q[qb]], writes=[psr[zb]])

        def s2a(b):
            c0 = b["c0"]
            zb = ZB[b["idx"] % 3]
            eb = b["idx"] % NE
            fw.op("act", lambda e: e.activation(out=E[eb][:, c0:512], in_=ps[zb][:, c0:512], func=AF.Exp),
                  reads=[psr[zb]], writes=[r_E[eb]])

        def s2b(b):
            c0, o = b["c0"], b["o"]
            eb = b["idx"] % NE
            lb = b["idx"] % NL
            fw.op("act", lambda e: e.activation(out=L[lb][:, c0:512], in_=E[eb][:, c0:512], func=AF.Ln,
                                                bias=1.0, scale=1.0),
                  reads=[r_E[eb]], writes=[r_L[lb]])
            if b["diag"]:
                fw.op("pool", lambda e: e.tensor_tensor(out=L[lb][:, o:o + 128], in0=L[lb][:, o:o + 128],
                                                        in1=mtri[:], op=ALU.mult),
                      reads=[r_L[lb], r_c], writes=[r_L[lb]])

        def s3(b):
            nonlocal ri
            c0 = b["c0"]
            zb = ZB[b["idx"] % 3]
            lb = b["idx"] % NL
            first, last = b["first"], b["last"]
            if first:
                fw.op("pool", lambda e: e.memset(R[0][:], 0.0), writes=[r_R[0]])
                fw.op("pool", lambda e: e.memset(R[1][:], 0.0), writes=[r_R[1]])
                ri = 0
            fw.op("pe", lambda e: e.matmul(ps[zb][:, c0:512], lhsT=uneg[:], rhs=L[lb][:, c0:512],
                                           start=False, stop=first, skip_group_check=True),
                  reads=[r_L[lb], r_c], writes=[psr[zb]])
            if not first:
                fw.op("pe", lambda e: e.matmul(ps[zb][:, c0:512], lhsT=onesneg[:], rhs=R[ri][:, c0:512],
                                               start=False, stop=True, skip_group_check=True),
                      reads=[r_R[ri], r_c], writes=[psr[zb]])
            if not last:
                rn = 1 - ri
                fw.op("pool", lambda e: e.tensor_tensor(out=R[rn][:, c0:512], in0=R[ri][:, c0:512],
                                                        in1=L[lb][:, c0:512], op=ALU.add),
                      reads=[r_R[ri], r_L[lb]], writes=[r_R[rn]])
                ri = rn

        def s4(b):
            nonlocal wi
            c0, o = b["c0"], b["o"]
            zb = ZB[b["idx"] % 3]
            if b["first"]:
                wt, r_wt = Wf[b["qi"] % 2], r_Wf[b["qi"] % 2]
            else:
                wt, r_wt = W[wi % NW], r_W[wi % NW]
                wi += 1
            b["wt"], b["r_wt"] = wt, r_wt
            fw.op("act", lambda e: e.activation(out=wt[:, c0:512], in_=ps[zb][:, c0:512], func=AF.Exp),
                  reads=[psr[zb]], writes=[r_wt], partial=b["first"])
            if b["diag"]:
                fw.op("pool", lambda e: e.tensor_tensor(out=wt[:, o:o + 128], in0=wt[:, o:o + 128],
                                                        in1=mtri[:], op=ALU.mult),
                      reads=[r_wt, r_c], writes=[r_wt])

        def s5(b, it):
            qi, kb = b["qi"], b["kb"]
            wt, r_wt = b["wt"], b["r_wt"]
            first, last = b["first"], b["last"]
            pc0 = 0 if first else b["c0"]
            fw.op("pe", lambda e: e.matmul(ps[OB][:, pc0:512], lhsT=V[:, kb, :], rhs=wt[:, pc0:512],
                                           start=first, stop=last, skip_group_check=True),
                  reads=[r_Vc[kb // VB], r_wt], writes=[psr[OB]])
            if not last:
                return
            nb = ZB[it % 3]
            fw.op("dve", lambda e: e.tensor_copy(out=o_sb[:], in_=ps[OB][:]), reads=[psr[OB]], writes=[r_osbf])
            fw.op("dve", lambda e: e.tensor_tensor(out=sq[:], in0=o_sb[:], in1=o_sb[:], op=ALU.mult),
                  reads=[r_osbf], writes=[r_sq])
            fw.op("pe", lambda e: e.matmul(ps[nb][:], lhsT=onesf[:], rhs=sq[:], start=True, stop=True),
                  reads=[r_sq, r_c], writes=[psr[nb]])
            fw.op("dve", lambda e: e.tensor_scalar(out=nv[:, 0, :], in0=ps[nb][:], scalar1=1.0 / 128.0, scalar2=EPS,
                                                   op0=ALU.mult, op1=ALU.add),
                  reads=[psr[nb]], writes=[r_nv])
            pending.append((it + 2, lambda cur: s5_fin(qi)))

        def s5_fin(qi):
            qb = qi % 2
            fw.op("act", lambda e: e.activation(out=nv[:, 1, :], in_=nv[:, 0, :], func=AF.Ln),
                  reads=[r_nv], writes=[r_nv])
            fw.op("act", lambda e: e.activation(out=nv[:, 0, :], in_=nv[:, 1, :], func=AF.Exp, scale=-0.5),
                  reads=[r_nv], writes=[r_nv])
            fw.op("dve", lambda e: e.scalar_tensor_tensor(
                out=osb[qb][:], in0=o_sb[:], scalar=gsb, in1=nv[:, 0, :], op0=ALU.mult, op1=ALU.mult),
                reads=[r_osbf, r_nv, r_l], writes=[r_osb[qb]])
            fw.dma("pool", mixT[0:128, qi * 512:(qi + 1) * 512], osb[qb][:], reads=[r_osb[qb]])

        def d1(b):
            qi, kb, c0 = b["qi"], b["kb"], b["c0"]
            qb = qi % 2
            if b["first"] and qi + 1 < NQ:
                fw.dma("sp", Qd[(qi + 1) % 2][:], qd_s[:, (qi + 1) * 512:(qi + 2) * 512],
                       writes=[r_Qd[(qi + 1) % 2]])
            fw.op("pe", lambda e: e.matmul(ps[ZD][:, c0:512], lhsT=KTd[0:64, kb * 128:(kb + 1) * 128],
                                           rhs=Qd[qb][0:64, c0:512], start=True, stop=True),
                  reads=[r_KTdc[kb // VB], r_Qd[qb]], writes=[psr[ZD]])
            fw.op("pe", lambda e: e.matmul(ps[ZD + 1][:, c0:512], lhsT=KTd[64:128, kb * 128:(kb + 1) * 128],
                                           rhs=Qd[qb][64:128, c0:512], start=True, stop=True),
                  reads=[r_KTdc[kb // VB], r_Qd[qb]], writes=[psr[ZD + 1]])

        def d2(b):
            nonlocal pi
            c0, o = b["c0"], b["o"]
            if b["first"]:
                pt, r_pt = Pf[b["qi"] % 2], r_Pf[b["qi"] % 2]
            else:
                pt, r_pt = P[pi % NP], r_P[pi % NP]
                pi += 1
            b["pt"], b["r_pt"] = pt, r_pt
            fw.op("act", lambda e: e.activation(out=pt[:, :, c0:512], in_=pp[ZD // 2][:, :, c0:512], func=AF.Exp),
                  reads=[psr[ZD], psr[ZD + 1]], writes=[r_pt], partial=b["first"])
            if b["diag"]:
                fw.op("pool", lambda e: e.memset(pt[64:128, :, o:o + 64], 0.0), reads=[r_pt], writes=[r_pt])

        def d3(b, it):
            qi, kb = b["qi"], b["kb"]
            pt, r_pt = b["pt"], b["r_pt"]
            first, last = b["first"], b["last"]
            c0 = b["c0"]
            pc0 = 0 if first else c0
            sbi = qi % 2
            for m in range(2):
                fw.op("pe", lambda e, m=m: e.matmul(ps[OD + m][:, pc0:512], lhsT=Vd[:, kb, :], rhs=pt[:, m, pc0:512],
                                                    start=first, stop=last, skip_group_check=True),
                      reads=[r_Vdc[kb // VB], r_pt], writes=[psr[OD + m]])
            for m, eng in ((0, "dve"), (1, "dve")):
                if first:
                    fw.op(eng, lambda e, m=m: e.tensor_copy(out=PS[sbi][:, m, :], in_=pt[:, m, :]),
                          reads=[r_pt], writes=[r_PS[sbi][m]])
                else:
                    fw.op(eng, lambda e, m=m: e.tensor_tensor(out=PS[sbi][:, m, c0:512], in0=PS[sbi][:, m, c0:512],
                                                              in1=pt[:, m, c0:512], op=ALU.add),
                          reads=[r_pt, r_PS[sbi][m]], writes=[r_PS[sbi][m]])
            if not last:
                return
            nb = ZB[it % 3]
            fw.op("dve", lambda e: e.tensor_copy(out=acc[:, 0, :], in_=ps[OD][:]),
                  reads=[psr[OD]], writes=[r_acc[0]])
            fw.op("dve", lambda e: e.tensor_copy(out=acc[:, 1, :], in_=ps[OD + 1][:]),
                  reads=[psr[OD + 1]], writes=[r_acc[1]])
            for m in range(2):
                fw.op("pe", lambda e, m=m: e.matmul(ps[nb][:], lhsT=onesf[:], rhs=PS[sbi][:, m, :],
                                                    start=True, stop=True),
                      reads=[r_c, r_PS[sbi][m]], writes=[psr[nb]])
                fw.op("dve", lambda e, m=m: e.tensor_copy(out=rc[:, m, :], in_=ps[nb][:]),
                      reads=[psr[nb]], writes=[r_rc[m]])
                fw.op("dve", lambda e, m=m: e.reciprocal(out=rc[:, m, :], in_=rc[:, m, :]),
                      reads=[r_rc[m]], writes=[r_rc[m]])
                fw.op("pool", lambda e, m=m: e.tensor_tensor(out=ab[:, m, :], in0=acc[:, m, :], in1=rc[:, m, :],
                                                             op=ALU.mult),
                      reads=[r_acc[m], r_rc[m]], writes=[r_ab], partial=(m > 0))
            fw.op("dve", lambda e: e.scalar_tensor_tensor(out=od[:], in0=ab[:, 1, :], scalar=neglam, in1=ab[:, 0, :],
                                                          op0=ALU.mult, op1=ALU.add),
                  reads=[r_ab, r_l], writes=[r_od])
            fw.op("pool", lambda e: e.tensor_tensor(out=sqd[:], in0=od[:], in1=od[:], op=ALU.mult),
                  reads=[r_od], writes=[r_sqd])
            pending.append((it + 2, lambda cur: d3_norm(qi, cur)))

        def d3_norm(qi, cur):
            nb = ZB[cur % 3]
            fw.op("pe", lambda e: e.matmul(ps[nb][:], lhsT=onesf[:], rhs=sqd[:], start=True, stop=True),
                  reads=[r_sqd, r_c], writes=[psr[nb]])
            fw.op("dve", lambda e: e.tensor_scalar(out=nvd[:, 0, :], in0=ps[nb][:], scalar1=1.0 / 128.0, scalar2=EPS,
                                                   op0=ALU.mult, op1=ALU.add),
                  reads=[psr[nb]], writes=[r_nvd])
            pending.append((cur + 2, lambda c2: d3_fin(qi)))

        def d3_fin(qi):
            qb = qi % 2
            fw.op("act", lambda e: e.activation(out=nvd[:, 1, :], in_=nvd[:, 0, :], func=AF.Ln),
                  reads=[r_nvd], writes=[r_nvd])
            fw.op("act", lambda e: e.activation(out=nvd[:, 0, :], in_=nvd[:, 1, :], func=AF.Exp, scale=-0.5),
                  reads=[r_nvd], writes=[r_nvd])
            fw.op("dve", lambda e: e.scalar_tensor_tensor(
                out=osbd[qb][:], in0=od[:], scalar=gdiff, in1=nvd[:, 0, :], op0=ALU.mult, op1=ALU.mult),
                reads=[r_od, r_nvd, r_l], writes=[r_osbd[qb]])
            fw.dma("pool", mixT[128:256, qi * 512:(qi + 1) * 512], osbd[qb][:], reads=[r_osbd[qb]])

        pending = []
        for i in range(-2, nbk + 2):
            if 0 <= i + 2 < nbk:
                s1(blocks[i + 2])
            if 0 <= i + 1 < nbk:
                s2a(blocks[i + 1])
            if 0 <= i < nbk:
                s4(blocks[i])
            if 0 <= i + 1 < nbk:
                s2b(blocks[i + 1])
            if 0 <= i - 1 < nbk:
                s5(blocks[i - 1], i)
                d3(blocks[i - 1], i)
            if 0 <= i + 1 < nbk:
                s3(blocks[i + 1])
            if 0 <= i < nbk:
                d2(blocks[i])
            if 0 <= i + 1 < nbk:
                d1(blocks[i + 1])
            pending.sort(key=lambda t: t[0])
            while pending and 0 <= i and (pending[0][0] < i or i >= nbk + 1):
                pending.pop(0)[1](i)
                pending.sort(key=lambda t: t[0])
        fw.barrier()
    fw.finish()
    return nc


def rope_tables(S):
    half = 8
    inv_freq = (np.float32(500000.0) ** (-np.arange(0, 16, 2, dtype=np.float32) / np.float32(16))).astype(np.float32)
    ang = (np.arange(S, dtype=np.float32)[:, None] * inv_freq[None, :]).astype(np.float32)
    cos = np.cos(ang).astype(np.float32).T
    sin = np.sin(ang).astype(np.float32).T
    C = np.ones((128, S), np.float32)
    Sn = np.zeros((128, S), np.float32)
    for m in range(2):
        C[m * 64:m * 64 + 8] = cos
        C[m * 64 + 8:m * 64 + 16] = cos
        Sn[m * 64:m * 64 + 8] = -sin
        Sn[m * 64 + 8:m * 64 + 16] = sin
    return C, Sn


def prep_wa(w_in_l, c):
    q_sb = w_in_l[:, c * 128:(c + 1) * 128]
    k_sb = w_in_l[:, 1024 + c * 128:1024 + (c + 1) * 128]
    v_sb = w_in_l[:, 2048 + c * 128:2048 + (c + 1) * 128]
    q_d = w_in_l[:, 3072 + c * 128:3072 + (c + 1) * 128]
    k_d = w_in_l[:, 4096 + c * 128:4096 + (c + 1) * 128]
    v_d = w_in_l[:, 5120 + c * 128:5120 + (c + 1) * 128]
    return np.ascontiguousarray(np.concatenate([q_sb, k_sb, q_d, k_d, v_sb, v_d], axis=1))


def rope_perm():
    perm = np.arange(128)
    for m in range(2):
        perm[m * 64:m * 64 + 8] = np.arange(m * 64 + 8, m * 64 + 16)
        perm[m * 64 + 8:m * 64 + 16] = np.arange(m * 64, m * 64 + 8)
    P = np.zeros((128, 128), np.float32)
    P[perm, np.arange(128)] = 1.0
    return P


def build_B(NT, D=2048, DFF=5632, final_name="xout"):
    nc = bass.Bass("TRN2", target_bir_lowering=False)
    KD = D // 128
    NF = DFF // 128
    NTT = NT // 128
    NTB = NT // 512
    dt = nc.dram_tensor
    mixt = dt("mixt", [NTT, 128, KD, 128], BF16, kind="ExternalInput").ap()
    x = dt("x", [NT, D], F32, kind="ExternalInput").ap()
    wo = dt("wo", [D, D], F32, kind="ExternalInput").ap()
    gains = dt("gains", [3, D], F32, kind="ExternalInput").ap()
    wg = dt("wg", [D, DFF], F32, kind="ExternalInput").ap()
    wu = dt("wu", [D, DFF], F32, kind="ExternalInput").ap()
    wd = dt("wd", [DFF, D], F32, kind="ExternalInput").ap()
    xout = dt(final_name, [NT, D], F32, kind="ExternalOutput").ap()
    x1s = dt("x1s", [NT, D], F32, kind="Internal").ap()
    h2s = dt("h2s", [NTB, 128, KD * 512], BF16, kind="Internal").ap()

    fw = FW(nc)
    es = fw.es
    ps = [es.enter_context(nc.psum_tensor("ps%d" % i, [128, 512], F32)) for i in range(8)]
    psr = [Res("ps%d" % i) for i in range(8)]

    ident = sb(es, nc, "ident", [128, 128], BF16)
    onesb = sb(es, nc, "onesb", [128, 128], BF16)
    r_ident = Res()
    fw.op("pool", lambda e: e.memset(onesb[:], 1.0), writes=[r_ident])
    fw.op("pool", lambda e: e.memset(ident[:], 0.0), writes=[r_ident])
    fw.op("pool", lambda e: e.affine_select(out=ident[:], in_=onesb[:], pattern=[[-1, 128]], base=0,
                                            channel_multiplier=1, compare_op=ALU.is_equal, fill=0.0),
          writes=[r_ident])

    with ExitStack() as e1:
        G = sb(e1, nc, "G", [128, 2, D], F32)
        r_G = Res()
        for i in range(2):
            fw.dma("sp", G[:, i, :], gains[i, :].partition_broadcast(128), writes=[r_G], partial=True)
        wob = sb(e1, nc, "wob", [128, KD, D], BF16)
        r_wo = [Res() for _ in range(4)]
        wo_v = wo.rearrange("(k p) c -> p k c", p=128)
        for cg in range(4):
            for k in range(0, KD, 4):
                fw.dma("pool", wob[:, k:k + 4, cg * 512:(cg + 1) * 512], wo_v[:, k:k + 4, cg * 512:(cg + 1) * 512],
                       writes=[r_wo[cg]], partial=True)
        NB = 2
        NLB = 3
        h2Tb = [sb(e1, nc, "h2Tb%d" % i, [128, KD, 512], BF16) for i in range(2)]
        r_h2Tb = [Res() for _ in range(2)]
        mt = [sb(e1, nc, "mt%d" % i, [128, KD, 128], BF16) for i in range(NLB)]
        xt = [sb(e1, nc, "xt%d" % i, [128, D], F32) for i in range(NLB)]
        h2 = [sb(e1, nc, "h2%d" % i, [128, D], BF16) for i in range(NB)]
        tmp = [sb(e1, nc, "tmp%d" % i, [128, 512], F32) for i in range(NB)]
        junk = sb(e1, nc, "junk", [128, 512], BF16)
        st = [sb(e1, nc, "st%d" % i, [128, 8], F32) for i in range(NB)]
        r_mt = [Res() for _ in range(NLB)]
        r_xt = [Res() for _ in range(NLB)]
        r_h2 = [Res() for _ in range(NB)]
        r_tmp = [Res() for _ in range(NB)]
        r_junk = Res()
        r_st = [Res() for _ in range(NB)]

        def load1(tt):
            lb = tt % NLB
            fw.dma("sp", mt[lb][:], mixt[tt], writes=[r_mt[lb]])
            fw.dma("sp", xt[lb][:], x[tt * 128:(tt + 1) * 128, :], writes=[r_xt[lb]])

        def mm1(tt):
            lb = tt % NLB
            pb = (tt % 2) * 4
            for cg in range(4):
                for k in range(KD):
                    fw.op("pe", lambda e, cg=cg, k=k: e.matmul(
                        ps[pb + cg][:], lhsT=mt[lb][:, k, :], rhs=wob[:, k, cg * 512:(cg + 1) * 512],
                        start=(k == 0), stop=(k == KD - 1)),
                        reads=[r_mt[lb], r_wo[cg]], writes=[psr[pb + cg]])

        load1(0)
        if NTT > 1:
            load1(1)
        mm1(0)
        for tt in range(NTT):
            b = tt % NB
            lb = tt % NLB
            pb = (tt % 2) * 4
            if tt + 2 < NTT:
                load1(tt + 2)
            if tt + 1 < NTT:
                mm1(tt + 1)
            for cg in range(4):
                fw.op("act", lambda e, cg=cg: e.activation(
                    out=junk[:, 0:512], in_=ps[pb + cg][:], func=AF.Square, accum_out=st[b][:, cg:cg + 1]),
                    reads=[psr[pb + cg]], writes=[r_junk, r_st[b]], partial=True)
            fw.op("dve", lambda e: e.tensor_reduce(out=st[b][:, 4:5], in_=st[b][:, 0:4], axis=AX.X, op=ALU.add),
                  reads=[r_st[b]], writes=[r_st[b]])
            fw.op("dve", lambda e: e.tensor_scalar(out=st[b][:, 5:6], in0=st[b][:, 4:5], scalar1=1.0 / D,
                                                   scalar2=EPS, op0=ALU.mult, op1=ALU.add),
                  reads=[r_st[b]], writes=[r_st[b]])
            fw.op("act", lambda e: e.activation(out=st[b][:, 6:7], in_=st[b][:, 5:6], func=AF.Sqrt),
                  reads=[r_st[b]], writes=[r_st[b]])
            fw.op("dve", lambda e: e.reciprocal(out=st[b][:, 7:8], in_=st[b][:, 6:7]),
                  reads=[r_st[b]], writes=[r_st[b]])
            for cg in range(4):
                fw.op("dve", lambda e, cg=cg: e.scalar_tensor_tensor(
                    out=tmp[cg % NB][:], in0=ps[pb + cg][:], scalar=st[b][:, 7:8], in1=G[:, 0, cg * 512:(cg + 1) * 512],
                    op0=ALU.mult, op1=ALU.mult),
                    reads=[psr[pb + cg], r_st[b], r_G], writes=[r_tmp[cg % NB]])
                fw.op("pool", lambda e, cg=cg: e.tensor_tensor(
                    out=xt[lb][:, cg * 512:(cg + 1) * 512], in0=tmp[cg % NB][:], in1=xt[lb][:, cg * 512:(cg + 1) * 512],
                    op=ALU.add),
                    reads=[r_tmp[cg % NB], r_xt[lb]], writes=[r_xt[lb]])
            fw.dma("pool", x1s[tt * 128:(tt + 1) * 128, :], xt[lb][:], reads=[r_xt[lb]])
            fw.op("act", lambda e: e.activation(out=h2[b][:], in_=xt[lb][:], func=AF.Square,
                                                accum_out=st[b][:, 0:1]),
                  reads=[r_xt[lb]], writes=[r_h2[b], r_st[b]])
            fw.op("dve", lambda e: e.tensor_scalar(out=st[b][:, 1:2], in0=st[b][:, 0:1], scalar1=1.0 / D,
                                                   scalar2=EPS, op0=ALU.mult, op1=ALU.add),
                  reads=[r_st[b]], writes=[r_st[b]])
            fw.op("act", lambda e: e.activation(out=st[b][:, 2:3], in_=st[b][:, 1:2], func=AF.Sqrt),
                  reads=[r_st[b]], writes=[r_st[b]])
            fw.op("dve", lambda e: e.reciprocal(out=st[b][:, 3:4], in_=st[b][:, 2:3]),
                  reads=[r_st[b]], writes=[r_st[b]])
            fw.op("dve", lambda e: e.scalar_tensor_tensor(
                out=h2[b][:], in0=xt[lb][:], scalar=st[b][:, 3:4], in1=G[:, 1, :], op0=ALU.mult, op1=ALU.mult),
                reads=[r_xt[lb], r_st[b], r_G], writes=[r_h2[b]])
            for hb in range(2):
                pv = ps[pb + hb][:].bitcast(BF16)
                for kk in range(8):
                    k = hb * 8 + kk
                    fw.op("pe", lambda e, k=k, kk=kk, pv=pv: e.transpose(
                        pv[:, kk * 128:(kk + 1) * 128], h2[b][:, k * 128:(k + 1) * 128], ident[:]),
                        reads=[r_h2[b], r_ident], writes=[psr[pb + hb]], partial=(kk > 0))
                fw.op("dve" if hb == 0 else "act",
                      (lambda e, hb=hb, pv=pv: e.tensor_copy(
                          out=h2Tb[(tt // 4) % 2][:, hb * 8:(hb + 1) * 8, (tt % 4) * 128:(tt % 4 + 1) * 128],
                          in_=pv.rearrange("p (k t) -> p k t", k=8))) if hb == 0 else
                      (lambda e, hb=hb, pv=pv: e.activation(
                          out=h2Tb[(tt // 4) % 2][:, hb * 8:(hb + 1) * 8, (tt % 4) * 128:(tt % 4 + 1) * 128],
                          in_=pv.rearrange("p (k t) -> p k t", k=8), func=AF.Copy)),
                      reads=[psr[pb + hb]], writes=[r_h2Tb[(tt // 4) % 2]], partial=not (hb == 0 and tt % 4 == 0))
            if tt % 4 == 3:
                fw.dma("pool", h2s[tt // 4], h2Tb[(tt // 4) % 2][:].rearrange("p k t -> p (k t)"),
                       reads=[r_h2Tb[(tt // 4) % 2]])
        fw.barrier()

    with ExitStack() as e2:
        G2 = sb(e2, nc, "G2", [128, D], F32)
        r_G = Res()
        fw.dma("sp", G2[:], gains[2, :].partition_broadcast(128), writes=[r_G])
        h2Tl = [sb(e2, nc, "h2Tl%d" % i, [128, KD, 512], BF16) for i in range(2)]
        r_h2Tl = [Res() for _ in range(2)]

        def load_h2(tb):
            fw.dma("sp", h2Tl[tb % 2][:].rearrange("p k t -> p (k t)"), h2s[tb], writes=[r_h2Tl[tb % 2]])

        load_h2(0)
        FS = 512
        NFS = DFF // FS
        NWB = 2
        wgb = [sb(e2, nc, "wgb%d" % i, [128, KD, FS], BF16) for i in range(NWB)]
        wub = [sb(e2, nc, "wub%d" % i, [128, KD, FS], BF16) for i in range(NWB)]
        r_wg = [Res() for _ in range(NWB)]
        r_wu = [Res() for _ in range(NWB)]
        NDB = 4
        wdb = [sb(e2, nc, "wdb%d" % i, [128, D // 2], BF16) for i in range(NDB)]
        r_wd = [Res() for _ in range(NDB)]
        yh = [sb(e2, nc, "yh%d" % i, [128, D // 2], F32) for i in range(4)]
        r_yh = [Res() for _ in range(4)]
        aT = sb(e2, nc, "aT", [128, NF, 512], BF16)
        r_aT = [Res() for _ in range(NF)]
        sg = [sb(e2, nc, "sg%d" % i, [128, 512], F32) for i in range(2)]
        r_sg = [Res() for _ in range(2)]
        x1b = [sb(e2, nc, "x1b%d" % i, [128, D], F32) for i in range(2)]
        r_x1b = [Res() for _ in range(2)]
        tmp2 = [sb(e2, nc, "tmpb%d" % i, [128, 512], F32) for i in range(2)]
        r_tmp2 = [Res() for _ in range(2)]
        junk2 = sb(e2, nc, "junk2", [128, D // 2], BF16)
        r_junk2 = Res()
        st2 = [sb(e2, nc, "st2%d" % i, [128, 8], F32) for i in range(2)]
        r_st2 = [Res() for _ in range(2)]
        wg_v = wg.rearrange("(k p) c -> p k c", p=128)
        wu_v = wu.rearrange("(k p) c -> p k c", p=128)

        gu_seq = [(tb, s) for tb in range(NTB) for s in range(NFS)]

        def load_gu(i):
            tb, s = gu_seq[i]
            b = i % NWB
            for k in range(0, KD, 4):
                fw.dma("pool", wgb[b][:, k:k + 4, :], wg_v[:, k:k + 4, s * FS:(s + 1) * FS], writes=[r_wg[b]],
                       partial=(k > 0))
            for k in range(0, KD, 4):
                fw.dma("pool", wub[b][:, k:k + 4, :], wu_v[:, k:k + 4, s * FS:(s + 1) * FS], writes=[r_wu[b]],
                       partial=(k > 0))

        wd_cnt = [0]

        def load_wd(f, ch):
            b = wd_cnt[0] % NDB
            wd_cnt[0] += 1
            fw.dma("pool", wdb[b][:], wd[f * 128:(f + 1) * 128, ch * (D // 2):(ch + 1) * (D // 2)],
                   writes=[r_wd[b]])
            return b

        gi = 0
        load_gu(0)
        pcount = 0
        for tb in range(NTB):
            if tb + 1 < NTB:
                load_h2(tb + 1)
            hcur, r_hcur = h2Tl[tb % 2], r_h2Tl[tb % 2]
            for s in range(NFS):
                b = gi % NWB
                if gi + 1 < len(gu_seq):
                    load_gu(gi + 1)
                gi += 1
                for c in range(FS // 128):
                    f = s * (FS // 128) + c
                    pg = (pcount % 2) * 2
                    pcount += 1
                    for k in range(KD):
                        fw.op("pe", lambda e, k=k, c=c, pg=pg: e.matmul(
                            ps[pg][:], lhsT=wgb[b][:, k, c * 128:(c + 1) * 128], rhs=hcur[:, k, :],
                            start=(k == 0), stop=(k == KD - 1)),
                            reads=[r_wg[b], r_hcur], writes=[psr[pg]])
                    for k in range(KD):
                        fw.op("pe", lambda e, k=k, c=c, pg=pg: e.matmul(
                            ps[pg + 1][:], lhsT=wub[b][:, k, c * 128:(c + 1) * 128], rhs=hcur[:, k, :],
                            start=(k == 0), stop=(k == KD - 1)),
                            reads=[r_wu[b], r_hcur], writes=[psr[pg + 1]])
                    sb_ = f % 2
                    fw.op("act", lambda e, pg=pg, sb_=sb_: e.activation(out=sg[sb_][:], in_=ps[pg][:], func=AF.Silu),
                          reads=[psr[pg]], writes=[r_sg[sb_]])
                    fw.op("dve", lambda e, pg=pg, sb_=sb_, f=f: e.tensor_tensor(
                        out=aT[:, f, :], in0=sg[sb_][:], in1=ps[pg + 1][:], op=ALU.mult),
                        reads=[r_sg[sb_], psr[pg + 1]], writes=[r_aT[f]])
            for ch in range(2):
                wb_next = load_wd(0, ch)
                for f in range(NF):
                    wb = wb_next
                    if f + 1 < NF:
                        wb_next = load_wd(f + 1, ch)
                    for t4 in range(4):
                        for c2 in range(2):
                            bnk = t4 * 2 + c2
                            fw.op("pe", lambda e, f=f, t4=t4, c2=c2, wb=wb, bnk=bnk: e.matmul(
                                ps[bnk][:], lhsT=aT[:, f, t4 * 128:(t4 + 1) * 128],
                                rhs=wdb[wb][:, c2 * 512:(c2 + 1) * 512], start=(f == 0), stop=(f == NF - 1)),
                                reads=[r_aT[f], r_wd[wb]], writes=[psr[bnk]])
                if ch == 0:
                    for t4 in range(4):
                        for c2 in range(2):
                            bnk = t4 * 2 + c2
                            if c2 == 0:
                                fw.op("dve", lambda e: e.tensor_copy(out=yh[t4][:, 0:512], in_=ps[bnk][:]),
                                      reads=[psr[bnk]], writes=[r_yh[t4]])
                            else:
                                fw.op("act", lambda e: e.activation(out=yh[t4][:, 512:1024], in_=ps[bnk][:],
                                                                    func=AF.Copy),
                                      reads=[psr[bnk]], writes=[r_yh[t4]], partial=True)
                    continue
                for t4 in range(4):
                    tt = tb * 4 + t4
                    ob = tt % 2
                    fw.dma("sp", x1b[ob][:], x1s[tt * 128:(tt + 1) * 128, :], writes=[r_x1b[ob]])
                    fw.op("act", lambda e: e.activation(out=junk2[:], in_=yh[t4][:], func=AF.Square,
                                                        accum_out=st2[ob][:, 0:1]),
                          reads=[r_yh[t4]], writes=[r_junk2, r_st2[ob]], partial=True)
                    for c2 in range(2):
                        fw.op("act", lambda e, c2=c2: e.activation(
                            out=junk2[:, 0:512], in_=ps[t4 * 2 + c2][:], func=AF.Square,
                            accum_out=st2[ob][:, 1 + c2:2 + c2]),
                            reads=[psr[t4 * 2 + c2]], writes=[r_junk2, r_st2[ob]], partial=True)
                    fw.op("dve", lambda e: e.tensor_reduce(out=st2[ob][:, 4:5], in_=st2[ob][:, 0:3], axis=AX.X,
                                                           op=ALU.add),
                          reads=[r_st2[ob]], writes=[r_st2[ob]])
                    fw.op("dve", lambda e: e.tensor_scalar(out=st2[ob][:, 5:6], in0=st2[ob][:, 4:5], scalar1=1.0 / D,
                                                           scalar2=EPS, op0=ALU.mult, op1=ALU.add),
                          reads=[r_st2[ob]], writes=[r_st2[ob]])
                    fw.op("act", lambda e: e.activation(out=st2[ob][:, 6:7], in_=st2[ob][:, 5:6], func=AF.Sqrt),
                          reads=[r_st2[ob]], writes=[r_st2[ob]])
                    fw.op("dve", lambda e: e.reciprocal(out=st2[ob][:, 7:8], in_=st2[ob][:, 6:7]),
                          reads=[r_st2[ob]], writes=[r_st2[ob]])
                    for cg in range(4):
                        if cg < 2:
                            src, rsrc = yh[t4][:, cg * 512:(cg + 1) * 512], r_yh[t4]
                        else:
                            src, rsrc = ps[t4 * 2 + cg - 2][:], psr[t4 * 2 + cg - 2]
                        fw.op("dve", lambda e, cg=cg, src=src: e.scalar_tensor_tensor(
                            out=tmp2[cg % 2][:], in0=src, scalar=st2[ob][:, 7:8],
                            in1=G2[:, cg * 512:(cg + 1) * 512], op0=ALU.mult, op1=ALU.mult),
                            reads=[rsrc, r_st2[ob], r_G], writes=[r_tmp2[cg % 2]])
                        fw.op("dve", lambda e, cg=cg: e.tensor_tensor(
                            out=x1b[ob][:, cg * 512:(cg + 1) * 512], in0=tmp2[cg % 2][:],
                            in1=x1b[ob][:, cg * 512:(cg + 1) * 512], op=ALU.add),
                            reads=[r_tmp2[cg % 2], r_x1b[ob]], writes=[r_x1b[ob]])
                    fw.dma("sp", xout[tt * 128:(tt + 1) * 128, :], x1b[ob][:], reads=[r_x1b[ob]])
        fw.barrier()
    fw.finish()
    return nc


S_FULL = 16384
D_MODEL = 2048
N_CORES = 8


def _lambda_init(layer):
    return 0.8 - 0.6 * math.exp(-0.3 * layer)


_PROGS = {}


def _prog(name):
    if name not in _PROGS:
        _PROGS[name] = build_A(S_FULL) if name == "A" else build_B(S_FULL // N_CORES)
    return _PROGS[name]


def kernel(x, w_in, w_o, sb_out_norm, diff_subln, lambda_q1, lambda_k1, lambda_q2, lambda_k2,
           pre_mix_norm, post_mix_norm, pre_ffn_norm, post_ffn_norm, w_gate, w_up, w_down):
    f32 = lambda a: np.ascontiguousarray(np.asarray(a, dtype=np.float32))
    xc = f32(x)[0]
    S, D = xc.shape
    NT = S // N_CORES
    C, Sn = rope_tables(S)
    PT = rope_perm()
    cores = list(range(N_CORES))
    for l in range(2):
        li = _lambda_init(l)
        consts = np.tile(np.array([[li, 1.0 - li]], np.float32), (128, 1))
        lamv = f32(np.stack([lambda_q1[l], lambda_k1[l], lambda_q2[l], lambda_k2[l]]))
        w_in_l = f32(w_in[l])
        in_maps = []
        for c in cores:
            in_maps.append(dict(
                x=xc, wa=prep_wa(w_in_l, c), gpre=f32(pre_mix_norm[l]),
                hg=f32(np.stack([sb_out_norm[l][c], diff_subln[l][c]], axis=1)),
                lamv=lamv, consts=consts, cos_t=C, sin_t=Sn, permT=PT))
        res = run_bass_kernel_spmd(_prog("A"), in_maps, core_ids=cores)
        mix = [np.asarray(r["mixT"]) for r in res.results]
        mixT = np.concatenate([m[0:128] for m in mix] + [m[128:256] for m in mix], axis=0)
        gains = f32(np.stack([post_mix_norm[l], pre_ffn_norm[l], post_ffn_norm[l]]))
        wo_l, wg_l, wu_l, wd_l = f32(w_o[l]), f32(w_gate[l]), f32(w_up[l]), f32(w_down[l])
        in_maps = []
        for i in cores:
            mt = mixT[:, i * NT:(i + 1) * NT].reshape(D // 128, 128, NT // 128, 128).transpose(2, 1, 0, 3)
            in_maps.append(dict(mixt=np.ascontiguousarray(mt), x=np.ascontiguousarray(xc[i * NT:(i + 1) * NT]),
                                wo=wo_l, gains=gains, wg=wg_l, wu=wu_l, wd=wd_l))
        res = run_bass_kernel_spmd(_prog("B"), in_maps, core_ids=cores)
        xc = np.concatenate([np.asarray(r["xout"]) for r in res.results], axis=0)
    return xc[None].astype(np.float32)
```

```python
import math
import numpy as np
import ml_dtypes
from contextlib import ExitStack
import concourse.bass as bass
import concourse.mybir as mybir
from concourse.bass_utils import run_bass_kernel_spmd

F32 = mybir.dt.float32
BF16 = mybir.dt.bfloat16
AF = mybir.ActivationFunctionType
ALU = mybir.AluOpType
AX = mybir.AxisListType
EPS = 1e-6
SAME_ENGINE_SYNC = True
_SKIP = set()


class Res:
    __slots__ = ("w", "r", "name")

    def __init__(self, name=""):
        self.w = {}
        self.r = {}
        self.name = name


class FW:
    def __init__(self, nc, n_dma_sems=6):
        self.nc = nc
        self.es = ExitStack()
        self.eng = {"pe": nc.tensor, "act": nc.scalar, "dve": nc.vector,
                    "pool": nc.gpsimd, "sp": nc.sync}
        self.sems = {}
        self.cnt = {}
        for k in self.eng:
            self._mksem(k)
        self.known = {k: {} for k in self.eng}
        self.dq = {}
        for q in ("sp", "pool", "act"):
            keys = []
            for j in range(n_dma_sems):
                key = "d_%s%d" % (q, j)
                self._mksem(key)
                keys.append(key)
            self.dq[q] = {"keys": keys, "i": 0}
        self.n_inst = 0

    def _mksem(self, key):
        self.sems[key] = self.es.enter_context(self.nc.semaphore("s_" + key))
        self.cnt[key] = 0

    def _wait(self, e, deps):
        for k, v in deps.items():
            if v <= 0 or self.known[e].get(k, 0) >= v:
                continue
            if k == e and (e == "pe" or not SAME_ENGINE_SYNC):
                continue
            self.eng[e].wait_ge(self.sems[k], v)
            self.known[e][k] = v

    @staticmethod
    def _add(deps, d):
        for k, v in d.items():
            if deps.get(k, 0) < v:
                deps[k] = v

    def _deps(self, reads, writes, partial):
        deps = {}
        for r in reads:
            self._add(deps, r.w)
        for w in writes:
            self._add(deps, w.r)
            if not partial:
                self._add(deps, w.w)
        return deps

    def _commit(self, key, val, reads, writes, partial):
        for r in reads:
            if r.r.get(key, 0) < val:
                r.r[key] = val
        for w in writes:
            if partial:
                if w.w.get(key, 0) < val:
                    w.w[key] = val
            else:
                w.w = {key: val}
            w.r = {}

    def op(self, e, fn, reads=(), writes=(), partial=False):
        deps = self._deps(reads, writes, partial)
        self._wait(e, deps)
        inst = fn(self.eng[e])
        self.cnt[e] += 1
        inst.then_inc(self.sems[e], 1)
        self._commit(e, self.cnt[e], reads, writes, partial)
        self.n_inst += 1
        return inst

    def dma(self, q, out, in_, reads=(), writes=(), partial=False, **kw):
        deps = self._deps(reads, writes, partial)
        pool = self.dq[q]
        key = pool["keys"][pool["i"] % len(pool["keys"])]
        pool["i"] += 1
        if deps.get(key, 0) < self.cnt[key]:
            deps[key] = self.cnt[key]
        self._wait(q, deps)
        inst = self.eng[q].dma_start(out=out, in_=in_, **kw)
        self.cnt[key] += 16
        inst.then_inc(self.sems[key], 16)
        self._commit(key, self.cnt[key], reads, writes, partial)
        self.n_inst += 1
        return inst

    def barrier(self, engines=None):
        for e in (engines or self.eng):
            self._wait(e, dict(self.cnt))

    def finish(self):
        self._wait("sp", dict(self.cnt))
        self.es.close()


def sb(es, nc, name, shape, dtype):
    return es.enter_context(nc.sbuf_tensor(name, list(shape), dtype))


def build_A(S, D=2048, out_name="mixT"):
    nc = bass.Bass("TRN2", target_bir_lowering=False)
    KD = D // 128
    NG = S // 512
    NBLK = S // 128
    dt = nc.dram_tensor
    x = dt("x", [S, D], F32, kind="ExternalInput").ap()
    wa = dt("wa", [D, 768], F32, kind="ExternalInput").ap()
    permT = dt("permT", [128, 128], F32, kind="ExternalInput").ap()
    gpre = dt("gpre", [D], F32, kind="ExternalInput").ap()
    hg = dt("hg", [128, 2], F32, kind="ExternalInput").ap()
    lamv = dt("lamv", [4, 64], F32, kind="ExternalInput").ap()
    consts = dt("consts", [128, 2], F32, kind="ExternalInput").ap()
    cos_t = dt("cos_t", [128, S], F32, kind="ExternalInput").ap()
    sin_t = dt("sin_t", [128, S], F32, kind="ExternalInput").ap()
    mixT = dt(out_name, [256, S], BF16, kind="ExternalOutput").ap()
    qsb_s = dt("qsb_s", [128, S], BF16, kind="Internal").ap()
    ksb_s = dt("ksb_s", [128, S], BF16, kind="Internal").ap()
    vsb_s = dt("vsb_s", [128, NBLK, 128], BF16, kind="Internal").ap()
    qd_s = dt("qd_s", [128, S], BF16, kind="Internal").ap()
    kd_s = dt("kd_s", [128, S], BF16, kind="Internal").ap()
    vd_s = dt("vd_s", [128, NBLK, 128], BF16, kind="Internal").ap()

    fw = FW(nc)
    es = fw.es
    pp = [es.enter_context(nc.psum_tensor("pp%d" % i, [128, 2, 512], F32)) for i in range(3)]
    ps = [pp[i // 2][:, i % 2, :] for i in range(6)]
    ps += [es.enter_context(nc.psum_tensor("ps%d" % i, [128, 512], F32)) for i in range(6, 8)]
    psr = [Res("ps%d" % i) for i in range(8)]

    ident = sb(es, nc, "ident", [128, 128], BF16)
    onesb = sb(es, nc, "onesb", [128, 128], BF16)
    onesneg = sb(es, nc, "onesneg", [128, 128], BF16)
    uneg = sb(es, nc, "uneg", [128, 128], BF16)
    mtri = sb(es, nc, "mtri", [128, 128], BF16)
    onesf = sb(es, nc, "onesf", [128, 128], F32)
    r_c = Res()
    fw.op("pool", lambda e: e.memset(onesb[:], 1.0), writes=[r_c])
    fw.op("pool", lambda e: e.memset(onesneg[:], -1.0), writes=[r_c])
    fw.op("pool", lambda e: e.memset(onesf[:], 1.0), writes=[r_c])
    fw.op("pool", lambda e: e.memset(ident[:], 0.0), writes=[r_c])
    fw.op("pool", lambda e: e.memset(uneg[:], 0.0), writes=[r_c])
    fw.op("pool", lambda e: e.memset(mtri[:], 0.0), writes=[r_c])
    fw.op("pool", lambda e: e.affine_select(out=ident[:], in_=onesb[:], pattern=[[-1, 128]], base=0,
                                            channel_multiplier=1, compare_op=ALU.is_equal, fill=0.0), writes=[r_c])
    fw.op("pool", lambda e: e.affine_select(out=uneg[:], in_=onesneg[:], pattern=[[-1, 128]], base=0,
                                            channel_multiplier=1, compare_op=ALU.is_ge, fill=0.0), writes=[r_c])
    fw.op("pool", lambda e: e.affine_select(out=mtri[:], in_=onesb[:], pattern=[[1, 128]], base=0,
                                            channel_multiplier=-1, compare_op=ALU.is_gt, fill=0.0), writes=[r_c])
    hgt = sb(es, nc, "hgt", [128, 2], F32)
    cst = sb(es, nc, "cst", [128, 2], F32)
    lamt = sb(es, nc, "lamt", [128, 4, 64], F32)
    lsc = sb(es, nc, "lsc", [128, 8], F32)
    ltmp = sb(es, nc, "ltmp", [128, 64], F32)
    r_l = Res()
    fw.dma("sp", hgt[:], hg, writes=[r_l], partial=True)
    fw.dma("sp", cst[:], consts, writes=[r_l], partial=True)
    for i in range(4):
        fw.dma("sp", lamt[:, i, :], lamv[i, :].partition_broadcast(128), writes=[r_l], partial=True)
    for i in range(2):
        fw.op("dve", lambda e, i=i: e.tensor_tensor(out=ltmp[:], in0=lamt[:, 2 * i, :], in1=lamt[:, 2 * i + 1, :],
                                                    op=ALU.mult), reads=[r_l], writes=[r_l])
        fw.op("dve", lambda e, i=i: e.tensor_reduce(out=lsc[:, i:i + 1], in_=ltmp[:], axis=AX.X, op=ALU.add),
              reads=[r_l], writes=[r_l])
        fw.op("act", lambda e, i=i: e.activation(out=lsc[:, 2 + i:3 + i], in_=lsc[:, i:i + 1], func=AF.Exp),
              reads=[r_l], writes=[r_l])
    fw.op("dve", lambda e: e.tensor_tensor(out=lsc[:, 6:7], in0=lsc[:, 3:4], in1=lsc[:, 2:3], op=ALU.subtract),
          reads=[r_l], writes=[r_l])
    fw.op("dve", lambda e: e.tensor_tensor(out=lsc[:, 4:5], in0=lsc[:, 6:7], in1=cst[:, 0:1], op=ALU.subtract),
          reads=[r_l], writes=[r_l])
    fw.op("dve", lambda e: e.tensor_tensor(out=lsc[:, 5:6], in0=hgt[:, 1:2], in1=cst[:, 1:2], op=ALU.mult),
          reads=[r_l], writes=[r_l])
    neglam = lsc[:, 4:5]
    gdiff = lsc[:, 5:6]
    gsb = hgt[:, 0:1]

    with ExitStack() as e1:
        G = sb(e1, nc, "G", [128, D], F32)
        r_G = Res()
        fw.dma("sp", G[:], gpre.partition_broadcast(128), writes=[r_G])
        wb = sb(e1, nc, "wb", [128, KD, 768], BF16)
        pmT = sb(e1, nc, "pmT", [128, 128], BF16)
        r_pm = Res()
        fw.dma("pool", pmT[:], permT, writes=[r_pm])
        qhl = [sb(e1, nc, "qhl%d" % i, [128, 2, 512], BF16) for i in range(2)]
        r_qhl = [Res() for _ in range(2)]
        qres = [sb(e1, nc, "qres%d" % i, [128, 512], F32) for i in range(2)]
        r_qres = [Res() for _ in range(2)]
        r_w = Res()
        wa_v = wa.rearrange("(k p) c -> p k c", p=128)
        for k in range(0, KD, 2):
            fw.dma("pool", wb[:, k:k + 2, :], wa_v[:, k:k + 2, :], writes=[r_w], partial=True)
        NXB = 3
        xt = [sb(e1, nc, "xt%d" % i, [128, D], F32) for i in range(NXB)]
        r_xt = [Res() for _ in range(NXB)]
        hb = [sb(e1, nc, "hb%d" % i, [128, D], BF16) for i in range(2)]
        r_hb = [Res() for _ in range(2)]
        junk = sb(e1, nc, "junk", [128, D], BF16)
        r_junk = Res()
        st = [sb(e1, nc, "st%d" % i, [128, 4], F32) for i in range(2)]
        r_st = [Res() for _ in range(2)]
        hT = [sb(e1, nc, "hT%d" % i, [128, KD, 512], BF16) for i in range(2)]
        r_hT = [[Res() for _ in range(4)] for _ in range(2)]
        cs = [sb(e1, nc, "cs%d" % i, [128, 2, 512], F32) for i in range(2)]
        r_cs = [Res() for _ in range(2)]
        qk_st = [sb(e1, nc, "qkst%d" % i, [128, 4, 512], BF16) for i in range(2)]
        r_qk = [[Res() for _ in range(4)] for _ in range(2)]
        v_st = [sb(e1, nc, "vst%d" % i, [128, 256], BF16) for i in range(2)]
        r_v = [Res() for _ in range(2)]
        t12 = [sb(e1, nc, "t12%d" % i, [128, 2, 512], F32) for i in range(2)]
        r_t12 = [Res() for _ in range(2)]

        def loadx(tt):
            b = tt % NXB
            fw.dma("sp", xt[b][:], x[tt * 128:(tt + 1) * 128, :], writes=[r_xt[b]])

        proj_banks = [0, 1, 2, 3, 4, 5]
        pbi = [0]

        def nextbank():
            bnk = proj_banks[pbi[0] % len(proj_banks)]
            pbi[0] += 1
            return bnk

        loadx(0)
        loadx(1)
        vcnt = 0
        t12c = 0

        def chain(g, t):
            gb = g % 2
            if t == 0:
                fw.dma("sp", cs[gb][:, 0, :], cos_t[:, g * 512:(g + 1) * 512], writes=[r_cs[gb]])
                fw.dma("sp", cs[gb][:, 1, :], sin_t[:, g * 512:(g + 1) * 512], writes=[r_cs[gb]], partial=True)
            tt = g * 4 + t
            b = tt % NXB
            b2 = tt % 2
            if tt + 2 < NBLK:
                loadx(tt + 2)
            fw.op("act", lambda e: e.activation(out=junk[:], in_=xt[b][:], func=AF.Square,
                                                accum_out=st[b2][:, 0:1]),
                  reads=[r_xt[b]], writes=[r_junk, r_st[b2]])
            fw.op("dve", lambda e: e.tensor_scalar(out=st[b2][:, 1:2], in0=st[b2][:, 0:1], scalar1=1.0 / D,
                                                   scalar2=EPS, op0=ALU.mult, op1=ALU.add),
                  reads=[r_st[b2]], writes=[r_st[b2]])
            fw.op("act", lambda e: e.activation(out=st[b2][:, 2:3], in_=st[b2][:, 1:2], func=AF.Sqrt),
                  reads=[r_st[b2]], writes=[r_st[b2]])
            fw.op("dve", lambda e: e.reciprocal(out=st[b2][:, 3:4], in_=st[b2][:, 2:3]),
                  reads=[r_st[b2]], writes=[r_st[b2]])
            fw.op("dve", lambda e: e.scalar_tensor_tensor(
                out=hb[b2][:], in0=xt[b][:], scalar=st[b2][:, 3:4], in1=G[:], op0=ALU.mult, op1=ALU.mult),
                reads=[r_xt[b], r_st[b2], r_G], writes=[r_hb[b2]])

        def trans(g, t):
            gb = g % 2
            b2 = (g * 4 + t) % 2
            for h in range(2):
                pv = ps[6 + h][:].bitcast(BF16)
                for kk in range(8):
                    k = h * 8 + kk
                    fw.op("pe", lambda e, k=k, kk=kk, pv=pv: e.transpose(
                        pv[:, kk * 128:(kk + 1) * 128], hb[b2][:, k * 128:(k + 1) * 128], ident[:]),
                        reads=[r_hb[b2], r_c], writes=[psr[6 + h]], partial=(kk > 0))
                if h == 0:
                    fw.op("dve", lambda e, pv=pv: e.tensor_copy(
                        out=hT[gb][:, 0:8, t * 128:(t + 1) * 128], in_=pv.rearrange("p (k t) -> p k t", k=8)),
                        reads=[psr[6]], writes=[r_hT[gb][t]])
                else:
                    fw.op("act", lambda e, pv=pv: e.activation(
                        out=hT[gb][:, 8:16, t * 128:(t + 1) * 128], in_=pv.rearrange("p (k t) -> p k t", k=8),
                        func=AF.Copy),
                        reads=[psr[7]], writes=[r_hT[gb][t]], partial=True)

        def projs(g, hooks):
            nonlocal vcnt, t12c
            gb = g % 2

            def hook(u):
                for fn in hooks.get(u, ()):
                    fn()

            def proj(col0):
                bnk = nextbank()
                for k in range(KD):
                    fw.op("pe", lambda e, k=k: e.matmul(
                        ps[bnk][:], lhsT=wb[:, k, col0:col0 + 128], rhs=hT[gb][:, k, :],
                        start=(k == 0), stop=(k == KD - 1)),
                        reads=[r_w] + r_hT[gb], writes=[psr[bnk]])
                return bnk

            hook(-1)
            bq = proj(0)
            fw.op("act", lambda e: e.activation(out=qk_st[gb][:, 0, :], in_=ps[bq][:], func=AF.Copy,
                                                scale=1.0 / math.sqrt(128.0)),
                  reads=[psr[bq]], writes=[r_qk[gb][0]])
            fw.dma("pool", qsb_s[:, g * 512:(g + 1) * 512], qk_st[gb][:, 0, :], reads=[r_qk[gb][0]])
            hook(0)
            bk = proj(128)
            fw.op("dve", lambda e: e.tensor_copy(out=qk_st[gb][:, 1, :], in_=ps[bk][:]),
                  reads=[psr[bk]], writes=[r_qk[gb][1]])
            fw.dma("pool", ksb_s[:, g * 512:(g + 1) * 512], qk_st[gb][:, 1, :], reads=[r_qk[gb][1]])
            hook(1)
            late = []
            for idx, (c_a, scale, dst) in enumerate(((256, 0.125, qd_s), (384, 1.0, kd_s))):
                ba = proj(c_a)
                tb_ = t12c % 2
                t12c += 1
                fw.op("act", lambda e: e.activation(out=qhl[tb_][:, 0, :], in_=ps[ba][:], func=AF.Copy),
                      reads=[psr[ba]], writes=[r_qhl[tb_]])
                fw.op("pool", lambda e: e.tensor_copy(out=qres[tb_][:], in_=qhl[tb_][:, 0, :]),
                      reads=[r_qhl[tb_]], writes=[r_qres[tb_]])
                fw.op("dve", lambda e: e.tensor_tensor(out=qhl[tb_][:, 1, :], in0=ps[ba][:], in1=qres[tb_][:],
                                                       op=ALU.subtract),
                      reads=[psr[ba], r_qres[tb_]], writes=[r_qhl[tb_]], partial=True)
                fw.op("dve", lambda e: e.scalar_tensor_tensor(
                    out=t12[tb_][:, 0, :], in0=ps[ba][:], scalar=scale, in1=cs[gb][:, 0, :],
                    op0=ALU.mult, op1=ALU.mult),
                    reads=[psr[ba], r_cs[gb]], writes=[r_t12[tb_]])
                late.append((idx, tb_, scale, dst))
                hook(2 + idx)
            for t in range(4):
                tt = g * 4 + t
                bnk = nextbank()
                for k in range(KD):
                    fw.op("pe", lambda e, k=k: e.matmul(
                        ps[bnk][:, 0:256], lhsT=hT[gb][:, k, t * 128:(t + 1) * 128], rhs=wb[:, k, 512:768],
                        start=(k == 0), stop=(k == KD - 1)),
                        reads=[r_w, r_hT[gb][t]], writes=[psr[bnk]])
                vb = vcnt % 2
                vcnt += 1
                fw.op("dve" if t % 2 == 0 else "act",
                      (lambda e: e.tensor_copy(out=v_st[vb][:], in_=ps[bnk][:, 0:256])) if t % 2 == 0 else
                      (lambda e: e.activation(out=v_st[vb][:], in_=ps[bnk][:, 0:256], func=AF.Copy)),
                      reads=[psr[bnk]], writes=[r_v[vb]])
                fw.dma("pool", vsb_s[:, tt, :], v_st[vb][:, 0:128], reads=[r_v[vb]])
                fw.dma("pool", vd_s[:, tt, :], v_st[vb][:, 128:256], reads=[r_v[vb]])
                hook(4 + t)
            for idx, tb_, scale, dst in late:
                bp = nextbank()
                for j in range(2):
                    fw.op("pe", lambda e, j=j: e.matmul(ps[bp][:], lhsT=pmT[:], rhs=qhl[tb_][:, j, :],
                                                        start=(j == 0), stop=(j == 1)),
                          reads=[r_pm, r_qhl[tb_]], writes=[psr[bp]])
                fw.op("dve", lambda e: e.scalar_tensor_tensor(
                    out=t12[tb_][:, 1, :], in0=ps[bp][:], scalar=scale, in1=cs[gb][:, 1, :],
                    op0=ALU.mult, op1=ALU.mult),
                    reads=[psr[bp], r_cs[gb]], writes=[r_t12[tb_]], partial=True)
                fw.op("pool", lambda e: e.tensor_tensor(
                    out=qk_st[gb][:, 2 + idx, :], in0=t12[tb_][:, 0, :], in1=t12[tb_][:, 1, :], op=ALU.add),
                    reads=[r_t12[tb_]], writes=[r_qk[gb][2 + idx]])
                fw.dma("pool", dst[:, g * 512:(g + 1) * 512], qk_st[gb][:, 2 + idx, :], reads=[r_qk[gb][2 + idx]])

        for t in range(4):
            chain(0, t)
            trans(0, t)
        for g in range(NG):
            hooks = {}
            if g + 1 < NG:
                n = g + 1
                hooks = {-1: [lambda: chain(n, 0)],
                         0: [lambda: trans(n, 0), lambda: chain(n, 1)],
                         2: [lambda: trans(n, 1), lambda: chain(n, 2)],
                         3: [lambda: trans(n, 2), lambda: chain(n, 3)],
                         5: [lambda: trans(n, 3)]}
            projs(g, hooks)
        fw.barrier()

    NQ = S // 512
    blocks = []
    for qi in range(NQ):
        for kb in range(4 * qi + 3, -1, -1):
            o = kb * 128 - qi * 512
            blocks.append(dict(qi=qi, kb=kb, o=o, diag=(o >= 0), c0=max(o, 0), first=(kb == 4 * qi + 3),
                               last=(kb == 0), idx=len(blocks)))
    NBK = len(blocks)

    with ExitStack() as e2:
        KT = sb(e2, nc, "KT", [128, S], BF16)
        V = sb(e2, nc, "V", [128, NBLK, 128], BF16)
        KTd = sb(e2, nc, "KTd", [128, S], BF16)
        Vd = sb(e2, nc, "Vd", [128, NBLK, 128], BF16)
        CH = min(2048, S)
        VB = CH // 128
        NCH = S // CH
        r_KTc = [Res() for _ in range(NCH)]
        r_Vc = [Res() for _ in range(NCH)]
        r_KTdc = [Res() for _ in range(NCH)]
        r_Vdc = [Res() for _ in range(NCH)]
        for ci in range(NCH):
            c0, b0 = ci * CH, ci * VB
            fw.dma("sp", KT[:, c0:c0 + CH], ksb_s[:, c0:c0 + CH], writes=[r_KTc[ci]])
            fw.dma("sp", V[:, b0:b0 + VB, :], vsb_s[:, b0:b0 + VB, :], writes=[r_Vc[ci]])
            fw.dma("sp", KTd[:, c0:c0 + CH], kd_s[:, c0:c0 + CH], writes=[r_KTdc[ci]])
            fw.dma("sp", Vd[:, b0:b0 + VB, :], vd_s[:, b0:b0 + VB, :], writes=[r_Vdc[ci]])
        Q = [sb(e2, nc, "Q%d" % i, [128, 512], BF16) for i in range(2)]
        r_Q = [Res() for _ in range(2)]
        Qd = [sb(e2, nc, "Qd%d" % i, [128, 512], BF16) for i in range(2)]
        r_Qd = [Res() for _ in range(2)]
        NE, NL, NW = 2, 3, 3
        E = [sb(e2, nc, "E%d" % i, [128, 512], F32) for i in range(NE)]
        r_E = [Res() for _ in range(NE)]
        L = [sb(e2, nc, "L%d" % i, [128, 512], BF16) for i in range(NL)]
        r_L = [Res() for _ in range(NL)]
        R = [sb(e2, nc, "R%d" % i, [128, 512], BF16) for i in range(2)]
        r_R = [Res() for _ in range(2)]
        W = [sb(e2, nc, "W%d" % i, [128, 512], BF16) for i in range(NW)]
        r_W = [Res() for _ in range(NW)]
        Wf = [sb(e2, nc, "Wf%d" % i, [128, 512], BF16) for i in range(2)]
        r_Wf = [Res() for _ in range(2)]
        o_sb = sb(e2, nc, "o_sb", [128, 512], F32)
        r_osbf = Res()
        sq = sb(e2, nc, "sq", [128, 512], F32)
        r_sq = Res()
        nv = sb(e2, nc, "nv", [128, 2, 512], F32)
        r_nv = Res()
        osb = [sb(e2, nc, "osb%d" % i, [128, 512], BF16) for i in range(2)]
        r_osb = [Res() for _ in range(2)]
        NP = 3
        P = [sb(e2, nc, "P%d" % i, [128, 2, 512], BF16) for i in range(NP)]
        r_P = [Res() for _ in range(NP)]
        Pf = [sb(e2, nc, "Pf%d" % i, [128, 2, 512], BF16) for i in range(2)]
        r_Pf = [Res() for _ in range(2)]
        PS = [sb(e2, nc, "PS%d" % i, [128, 2, 512], F32) for i in range(2)]
        r_PS = [[Res() for _ in range(2)] for _ in range(2)]
        acc = sb(e2, nc, "acc", [128, 2, 512], F32)
        r_acc = [Res() for _ in range(2)]
        rc = sb(e2, nc, "rc", [128, 2, 512], F32)
        r_rc = [Res() for _ in range(2)]
        ab = sb(e2, nc, "ab", [128, 2, 512], F32)
        r_ab = Res()
        od = sb(e2, nc, "od", [128, 512], F32)
        r_od = Res()
        sqd = sb(e2, nc, "sqd", [128, 512], F32)
        r_sqd = Res()
        nvd = sb(e2, nc, "nvd", [128, 2, 512], F32)
        r_nvd = Res()
        osbd = [sb(e2, nc, "osbd%d" % i, [128, 512], BF16) for i in range(2)]
        r_osbd = [Res() for _ in range(2)]
        for i in range(2):
            fw.op("pool", lambda e, i=i: e.memset(Wf[i][:], 0.0), writes=[r_Wf[i]])
            fw.op("pool", lambda e, i=i: e.memset(Pf[i][:], 0.0), writes=[r_Pf[i]])
        ZB = [0, 1, 2]
        OB = 3
        ZD = 4
        OD = 6
        nbk = 0 if 'A2' in _SKIP else NBK
        if nbk:
            fw.dma("sp", Q[0][:], qsb_s[:, 0:512], writes=[r_Q[0]])
            fw.dma("sp", Qd[0][:], qd_s[:, 0:512], writes=[r_Qd[0]])
        ri = 0
        wi = 0
        pi = 0

        def s1(b):
            qi, kb, c0 = b["qi"], b["kb"], b["c0"]
            qb = qi % 2
            if b["first"] and qi + 1 < NQ:
                fw.dma("sp", Q[(qi + 1) % 2][:], qsb_s[:, (qi + 1) * 512:(qi + 2) * 512],
                       writes=[r_Q[(qi + 1) % 2]])
            zb = ZB[b["idx"] % 3]
            fw.op("pe", lambda e: e.matmul(ps[zb][:, c0:512], lhsT=KT[:, kb * 128:(kb + 1) * 128],
                                           rhs=Q[qb][:, c0:512], start=True, stop=True),
                  reads=[r_KTc[kb // VB], r_Q[qb]], writes=[psr[zb]])

        def s2a(b):
            c0 = b["c0"]
            zb = ZB[b["idx"] % 3]
            eb = b["idx"] % NE
            fw.op("act", lambda e: e.activation(out=E[eb][:, c0:512], in_=ps[zb][:, c0:512], func=AF.Exp),
                  reads=[psr[zb]], writes=[r_E[eb]])

        def s2b(b):
            c0, o = b["c0"], b["o"]
            eb = b["idx"] % NE
            lb = b["idx"] % NL
            fw.op("act", lambda e: e.activation(out=L[lb][:, c0:512], in_=E[eb][:, c0:512], func=AF.Ln,
                                                bias=1.0, scale=1.0),
                  reads=[r_E[eb]], writes=[r_L[lb]])
            if b["diag"]:
                fw.op("dve", lambda e: e.tensor_tensor(out=L[lb][:, o:o + 128], in0=L[lb][:, o:o + 128],
                                                        in1=mtri[:], op=ALU.mult),
                      reads=[r_L[lb], r_c], writes=[r_L[lb]])

        def s3(b):
            nonlocal ri
            c0 = b["c0"]
            zb = ZB[b["idx"] % 3]
            lb = b["idx"] % NL
            first, last = b["first"], b["last"]
            if first:
                fw.op("pool", lambda e: e.memset(R[0][:], 0.0), writes=[r_R[0]])
                fw.op("pool", lambda e: e.memset(R[1][:], 0.0), writes=[r_R[1]])
                ri = 0
            fw.op("pe", lambda e: e.matmul(ps[zb][:, c0:512], lhsT=uneg[:], rhs=L[lb][:, c0:512],
                                           start=False, stop=first, skip_group_check=True),
                  reads=[r_L[lb], r_c], writes=[psr[zb]])
            if not first:
                fw.op("pe", lambda e: e.matmul(ps[zb][:, c0:512], lhsT=onesneg[:], rhs=R[ri][:, c0:512],
                                               start=False, stop=True, skip_group_check=True),
                      reads=[r_R[ri], r_c], writes=[psr[zb]])
            if not last:
                rn = 1 - ri
                fw.op("pool", lambda e: e.tensor_tensor(out=R[rn][:, c0:512], in0=R[ri][:, c0:512],
                                                        in1=L[lb][:, c0:512], op=ALU.add),
                      reads=[r_R[ri], r_L[lb]], writes=[r_R[rn]])
                ri = rn

        def s4(b):
            nonlocal wi
            c0, o = b["c0"], b["o"]
            zb = ZB[b["idx"] % 3]
            if b["first"]:
                wt, r_wt = Wf[b["qi"] % 2], r_Wf[b["qi"] % 2]
            else:
                wt, r_wt = W[wi % NW], r_W[wi % NW]
                wi += 1
            b["wt"], b["r_wt"] = wt, r_wt
            fw.op("act", lambda e: e.activation(out=wt[:, c0:512], in_=ps[zb][:, c0:512], func=AF.Exp),
                  reads=[psr[zb]], writes=[r_wt], partial=b["first"])
            if b["diag"]:
                fw.op("dve", lambda e: e.tensor_tensor(out=wt[:, o:o + 128], in0=wt[:, o:o + 128],
                                                        in1=mtri[:], op=ALU.mult),
                      reads=[r_wt, r_c], writes=[r_wt])

        def s5(b, it):
            qi, kb = b["qi"], b["kb"]
            wt, r_wt = b["wt"], b["r_wt"]
            first, last = b["first"], b["last"]
            pc0 = 0 if first else b["c0"]
            fw.op("pe", lambda e: e.matmul(ps[OB][:, pc0:512], lhsT=V[:, kb, :], rhs=wt[:, pc0:512],
                                           start=first, stop=last, skip_group_check=True),
                  reads=[r_Vc[kb // VB], r_wt], writes=[psr[OB]])
            if not last:
                return
            nb = ZB[it % 3]
            fw.op("dve", lambda e: e.tensor_copy(out=o_sb[:], in_=ps[OB][:]), reads=[psr[OB]], writes=[r_osbf])
            fw.op("dve", lambda e: e.tensor_tensor(out=sq[:], in0=o_sb[:], in1=o_sb[:], op=ALU.mult),
                  reads=[r_osbf], writes=[r_sq])
            fw.op("pe", lambda e: e.matmul(ps[nb][:], lhsT=onesf[:], rhs=sq[:], start=True, stop=True),
                  reads=[r_sq, r_c], writes=[psr[nb]])
            fw.op("dve", lambda e: e.tensor_scalar(out=nv[:, 0, :], in0=ps[nb][:], scalar1=1.0 / 128.0, scalar2=EPS,
                                                   op0=ALU.mult, op1=ALU.add),
                  reads=[psr[nb]], writes=[r_nv])
            pending.append((it + 2, lambda cur: s5_fin(qi)))

        def s5_fin(qi):
            qb = qi % 2
            fw.op("act", lambda e: e.activation(out=nv[:, 1, :], in_=nv[:, 0, :], func=AF.Ln),
                  reads=[r_nv], writes=[r_nv])
            fw.op("act", lambda e: e.activation(out=nv[:, 0, :], in_=nv[:, 1, :], func=AF.Exp, scale=-0.5),
                  reads=[r_nv], writes=[r_nv])
            fw.op("dve", lambda e: e.scalar_tensor_tensor(
                out=osb[qb][:], in0=o_sb[:], scalar=gsb, in1=nv[:, 0, :], op0=ALU.mult, op1=ALU.mult),
                reads=[r_osbf, r_nv, r_l], writes=[r_osb[qb]])
            fw.dma("pool", mixT[0:128, qi * 512:(qi + 1) * 512], osb[qb][:], reads=[r_osb[qb]])

        def d1(b):
            qi, kb, c0 = b["qi"], b["kb"], b["c0"]
            qb = qi % 2
            if b["first"] and qi + 1 < NQ:
                fw.dma("sp", Qd[(qi + 1) % 2][:], qd_s[:, (qi + 1) * 512:(qi + 2) * 512],
                       writes=[r_Qd[(qi + 1) % 2]])
            fw.op("pe", lambda e: e.matmul(ps[ZD][:, c0:512], lhsT=KTd[0:64, kb * 128:(kb + 1) * 128],
                                           rhs=Qd[qb][0:64, c0:512], start=True, stop=True),
                  reads=[r_KTdc[kb // VB], r_Qd[qb]], writes=[psr[ZD]])
            fw.op("pe", lambda e: e.matmul(ps[ZD + 1][:, c0:512], lhsT=KTd[64:128, kb * 128:(kb + 1) * 128],
                                           rhs=Qd[qb][64:128, c0:512], start=True, stop=True),
                  reads=[r_KTdc[kb // VB], r_Qd[qb]], writes=[psr[ZD + 1]])

        def d2(b):
            nonlocal pi
            c0, o = b["c0"], b["o"]
            if b["first"]:
                pt, r_pt = Pf[b["qi"] % 2], r_Pf[b["qi"] % 2]
            else:
                pt, r_pt = P[pi % NP], r_P[pi % NP]
                pi += 1
            b["pt"], b["r_pt"] = pt, r_pt
            fw.op("act", lambda e: e.activation(out=pt[:, :, c0:512], in_=pp[ZD // 2][:, :, c0:512], func=AF.Exp),
                  reads=[psr[ZD], psr[ZD + 1]], writes=[r_pt], partial=b["first"])
            if b["diag"]:
                fw.op("dve", lambda e: e.memset(pt[64:128, :, o:o + 64], 0.0), reads=[r_pt], writes=[r_pt])

        def d3(b, it):
            qi, kb = b["qi"], b["kb"]
            pt, r_pt = b["pt"], b["r_pt"]
            first, last = b["first"], b["last"]
            c0 = b["c0"]
            pc0 = 0 if first else c0
            sbi = qi % 2
            for m in range(2):
                fw.op("pe", lambda e, m=m: e.matmul(ps[OD + m][:, pc0:512], lhsT=Vd[:, kb, :], rhs=pt[:, m, pc0:512],
                                                    start=first, stop=last, skip_group_check=True),
                      reads=[r_Vdc[kb // VB], r_pt], writes=[psr[OD + m]])
            for m, eng in ((0, "dve"), (1, "dve")):
                if first:
                    fw.op(eng, lambda e, m=m: e.tensor_copy(out=PS[sbi][:, m, :], in_=pt[:, m, :]),
                          reads=[r_pt], writes=[r_PS[sbi][m]])
                else:
                    fw.op(eng, lambda e, m=m: e.tensor_tensor(out=PS[sbi][:, m, c0:512], in0=PS[sbi][:, m, c0:512],
                                                              in1=pt[:, m, c0:512], op=ALU.add),
                          reads=[r_pt, r_PS[sbi][m]], writes=[r_PS[sbi][m]])
            if not last:
                return
            nb = ZB[it % 3]
            fw.op("dve", lambda e: e.tensor_copy(out=acc[:, 0, :], in_=ps[OD][:]),
                  reads=[psr[OD]], writes=[r_acc[0]])
            fw.op("dve", lambda e: e.tensor_copy(out=acc[:, 1, :], in_=ps[OD + 1][:]),
                  reads=[psr[OD + 1]], writes=[r_acc[1]])
            for m in range(2):
                fw.op("pe", lambda e, m=m: e.matmul(ps[OD + m][:], lhsT=onesf[:], rhs=PS[sbi][:, m, :],
                                                    start=True, stop=True),
                      reads=[r_c, r_PS[sbi][m]], writes=[psr[OD + m]])
                fw.op("dve", lambda e, m=m: e.tensor_copy(out=rc[:, m, :], in_=ps[OD + m][:]),
                      reads=[psr[OD + m]], writes=[r_rc[m]])
                fw.op("dve", lambda e, m=m: e.reciprocal(out=rc[:, m, :], in_=rc[:, m, :]),
                      reads=[r_rc[m]], writes=[r_rc[m]])
                fw.op("pool", lambda e, m=m: e.tensor_tensor(out=ab[:, m, :], in0=acc[:, m, :], in1=rc[:, m, :],
                                                             op=ALU.mult),
                      reads=[r_acc[m], r_rc[m]], writes=[r_ab], partial=(m > 0))
            fw.op("dve", lambda e: e.scalar_tensor_tensor(out=od[:], in0=ab[:, 1, :], scalar=neglam, in1=ab[:, 0, :],
                                                          op0=ALU.mult, op1=ALU.add),
                  reads=[r_ab, r_l], writes=[r_od])
            fw.op("pool", lambda e: e.tensor_tensor(out=sqd[:], in0=od[:], in1=od[:], op=ALU.mult),
                  reads=[r_od], writes=[r_sqd])
            pending.append((it + 2, lambda cur: d3_norm(qi, cur)))

        def d3_norm(qi, cur):
            nb = ZB[cur % 3]
            fw.op("pe", lambda e: e.matmul(ps[nb][:], lhsT=onesf[:], rhs=sqd[:], start=True, stop=True),
                  reads=[r_sqd, r_c], writes=[psr[nb]])
            fw.op("dve", lambda e: e.tensor_scalar(out=nvd[:, 0, :], in0=ps[nb][:], scalar1=1.0 / 128.0, scalar2=EPS,
                                                   op0=ALU.mult, op1=ALU.add),
                  reads=[psr[nb]], writes=[r_nvd])
            pending.append((cur + 2, lambda c2: d3_fin(qi)))

        def d3_fin(qi):
            qb = qi % 2
            fw.op("act", lambda e: e.activation(out=nvd[:, 1, :], in_=nvd[:, 0, :], func=AF.Ln),
                  reads=[r_nvd], writes=[r_nvd])
            fw.op("act", lambda e: e.activation(out=nvd[:, 0, :], in_=nvd[:, 1, :], func=AF.Exp, scale=-0.5),
                  reads=[r_nvd], writes=[r_nvd])
            fw.op("dve", lambda e: e.scalar_tensor_tensor(
                out=osbd[qb][:], in0=od[:], scalar=gdiff, in1=nvd[:, 0, :], op0=ALU.mult, op1=ALU.mult),
                reads=[r_od, r_nvd, r_l], writes=[r_osbd[qb]])
            fw.dma("pool", mixT[128:256, qi * 512:(qi + 1) * 512], osbd[qb][:], reads=[r_osbd[qb]])

        pending = []
        for i in range(-2, nbk + 2):
            if 0 <= i + 2 < nbk:
                s1(blocks[i + 2])
            if 0 <= i + 1 < nbk:
                s2a(blocks[i + 1])
            if 0 <= i < nbk:
                s4(blocks[i])
            if 0 <= i + 1 < nbk:
                s2b(blocks[i + 1])
            if 0 <= i - 1 < nbk:
                s5(blocks[i - 1], i)
                d3(blocks[i - 1], i)
            if 0 <= i + 1 < nbk:
                s3(blocks[i + 1])
            if 0 <= i < nbk:
                d2(blocks[i])
            if 0 <= i + 1 < nbk:
                d1(blocks[i + 1])
            pending.sort(key=lambda t: t[0])
            while pending and 0 <= i and (pending[0][0] < i or i >= nbk + 1):
                pending.pop(0)[1](i)
                pending.sort(key=lambda t: t[0])
        fw.barrier()
    fw.finish()
    return nc


def rope_tables(S):
    half = 8
    inv_freq = (np.float32(500000.0) ** (-np.arange(0, 16, 2, dtype=np.float32) / np.float32(16))).astype(np.float32)
    ang = (np.arange(S, dtype=np.float32)[:, None] * inv_freq[None, :]).astype(np.float32)
    cos = np.cos(ang).astype(np.float32).T
    sin = np.sin(ang).astype(np.float32).T
    C = np.ones((128, S), np.float32)
    Sn = np.zeros((128, S), np.float32)
    for m in range(2):
        C[m * 64:m * 64 + 8] = cos
        C[m * 64 + 8:m * 64 + 16] = cos
        Sn[m * 64:m * 64 + 8] = -sin
        Sn[m * 64 + 8:m * 64 + 16] = sin
    return C, Sn


def prep_wa(w_in_l, c):
    q_sb = w_in_l[:, c * 128:(c + 1) * 128]
    k_sb = w_in_l[:, 1024 + c * 128:1024 + (c + 1) * 128]
    v_sb = w_in_l[:, 2048 + c * 128:2048 + (c + 1) * 128]
    q_d = w_in_l[:, 3072 + c * 128:3072 + (c + 1) * 128]
    k_d = w_in_l[:, 4096 + c * 128:4096 + (c + 1) * 128]
    v_d = w_in_l[:, 5120 + c * 128:5120 + (c + 1) * 128]
    return np.ascontiguousarray(np.concatenate([q_sb, k_sb, q_d, k_d, v_sb, v_d], axis=1))


def rope_perm():
    perm = np.arange(128)
    for m in range(2):
        perm[m * 64:m * 64 + 8] = np.arange(m * 64 + 8, m * 64 + 16)
        perm[m * 64 + 8:m * 64 + 16] = np.arange(m * 64, m * 64 + 8)
    P = np.zeros((128, 128), np.float32)
    P[perm, np.arange(128)] = 1.0
    return P


def build_B(NT, D=2048, DFF=5632, final_name="xout"):
    nc = bass.Bass("TRN2", target_bir_lowering=False)
    KD = D // 128
    NF = DFF // 128
    NTT = NT // 128
    NTB = NT // 512
    dt = nc.dram_tensor
    mixt = dt("mixt", [NTT, 128, KD, 128], BF16, kind="ExternalInput").ap()
    x = dt("x", [NT, D], F32, kind="ExternalInput").ap()
    wo = dt("wo", [D, D], F32, kind="ExternalInput").ap()
    gains = dt("gains", [3, D], F32, kind="ExternalInput").ap()
    wg = dt("wg", [D, DFF], F32, kind="ExternalInput").ap()
    wu = dt("wu", [D, DFF], F32, kind="ExternalInput").ap()
    wd = dt("wd", [DFF, D], F32, kind="ExternalInput").ap()
    xout = dt(final_name, [NT, D], F32, kind="ExternalOutput").ap()
    x1s = dt("x1s", [NT, D], F32, kind="Internal").ap()
    h2s = dt("h2s", [NTB, 128, KD * 512], BF16, kind="Internal").ap()

    fw = FW(nc)
    es = fw.es
    ps = [es.enter_context(nc.psum_tensor("ps%d" % i, [128, 512], F32)) for i in range(8)]
    psr = [Res("ps%d" % i) for i in range(8)]

    ident = sb(es, nc, "ident", [128, 128], BF16)
    onesb = sb(es, nc, "onesb", [128, 128], BF16)
    r_ident = Res()
    fw.op("pool", lambda e: e.memset(onesb[:], 1.0), writes=[r_ident])
    fw.op("pool", lambda e: e.memset(ident[:], 0.0), writes=[r_ident])
    fw.op("pool", lambda e: e.affine_select(out=ident[:], in_=onesb[:], pattern=[[-1, 128]], base=0,
                                            channel_multiplier=1, compare_op=ALU.is_equal, fill=0.0),
          writes=[r_ident])

    with ExitStack() as e1:
        G = sb(e1, nc, "G", [128, 2, D], F32)
        r_G = Res()
        for i in range(2):
            fw.dma("sp", G[:, i, :], gains[i, :].partition_broadcast(128), writes=[r_G], partial=True)
        wob = sb(e1, nc, "wob", [128, KD, D], BF16)
        r_wo = [Res() for _ in range(4)]
        wo_v = wo.rearrange("(k p) c -> p k c", p=128)
        for cg in range(4):
            for k in range(0, KD, 4):
                fw.dma("pool", wob[:, k:k + 4, cg * 512:(cg + 1) * 512], wo_v[:, k:k + 4, cg * 512:(cg + 1) * 512],
                       writes=[r_wo[cg]], partial=True)
        NB = 2
        NLB = 3
        h2Tb = [sb(e1, nc, "h2Tb%d" % i, [128, KD, 512], BF16) for i in range(2)]
        r_h2Tb = [Res() for _ in range(2)]
        mt = [sb(e1, nc, "mt%d" % i, [128, KD, 128], BF16) for i in range(NLB)]
        xt = [sb(e1, nc, "xt%d" % i, [128, D], F32) for i in range(NLB)]
        h2 = [sb(e1, nc, "h2%d" % i, [128, D], BF16) for i in range(NB)]
        tmp = [sb(e1, nc, "tmp%d" % i, [128, 512], F32) for i in range(NB)]
        junk = sb(e1, nc, "junk", [128, 512], BF16)
        st = [sb(e1, nc, "st%d" % i, [128, 8], F32) for i in range(NB)]
        r_mt = [Res() for _ in range(NLB)]
        r_xt = [Res() for _ in range(NLB)]
        r_h2 = [Res() for _ in range(NB)]
        r_tmp = [Res() for _ in range(NB)]
        r_junk = Res()
        r_st = [Res() for _ in range(NB)]

        def load1(tt):
            lb = tt % NLB
            fw.dma("sp", mt[lb][:], mixt[tt], writes=[r_mt[lb]])
            fw.dma("sp", xt[lb][:], x[tt * 128:(tt + 1) * 128, :], writes=[r_xt[lb]])

        def mm1(tt):
            lb = tt % NLB
            pb = (tt % 2) * 4
            for cg in range(4):
                for k in range(KD):
                    fw.op("pe", lambda e, cg=cg, k=k: e.matmul(
                        ps[pb + cg][:], lhsT=mt[lb][:, k, :], rhs=wob[:, k, cg * 512:(cg + 1) * 512],
                        start=(k == 0), stop=(k == KD - 1)),
                        reads=[r_mt[lb], r_wo[cg]], writes=[psr[pb + cg]])

        load1(0)
        if NTT > 1:
            load1(1)
        mm1(0)
        for tt in range(NTT):
            b = tt % NB
            lb = tt % NLB
            pb = (tt % 2) * 4
            if tt + 2 < NTT:
                load1(tt + 2)
            if tt + 1 < NTT:
                mm1(tt + 1)
            for cg in range(4):
                fw.op("act", lambda e, cg=cg: e.activation(
                    out=junk[:, 0:512], in_=ps[pb + cg][:], func=AF.Square, accum_out=st[b][:, cg:cg + 1]),
                    reads=[psr[pb + cg]], writes=[r_junk, r_st[b]], partial=True)
            fw.op("dve", lambda e: e.tensor_reduce(out=st[b][:, 4:5], in_=st[b][:, 0:4], axis=AX.X, op=ALU.add),
                  reads=[r_st[b]], writes=[r_st[b]])
            fw.op("dve", lambda e: e.tensor_scalar(out=st[b][:, 5:6], in0=st[b][:, 4:5], scalar1=1.0 / D,
                                                   scalar2=EPS, op0=ALU.mult, op1=ALU.add),
                  reads=[r_st[b]], writes=[r_st[b]])
            fw.op("act", lambda e: e.activation(out=st[b][:, 6:7], in_=st[b][:, 5:6], func=AF.Sqrt),
                  reads=[r_st[b]], writes=[r_st[b]])
            fw.op("dve", lambda e: e.reciprocal(out=st[b][:, 7:8], in_=st[b][:, 6:7]),
                  reads=[r_st[b]], writes=[r_st[b]])
            for cg in range(4):
                fw.op("dve", lambda e, cg=cg: e.scalar_tensor_tensor(
                    out=tmp[cg % NB][:], in0=ps[pb + cg][:], scalar=st[b][:, 7:8], in1=G[:, 0, cg * 512:(cg + 1) * 512],
                    op0=ALU.mult, op1=ALU.mult),
                    reads=[psr[pb + cg], r_st[b], r_G], writes=[r_tmp[cg % NB]])
                fw.op("pool", lambda e, cg=cg: e.tensor_tensor(
                    out=xt[lb][:, cg * 512:(cg + 1) * 512], in0=tmp[cg % NB][:], in1=xt[lb][:, cg * 512:(cg + 1) * 512],
                    op=ALU.add),
                    reads=[r_tmp[cg % NB], r_xt[lb]], writes=[r_xt[lb]])
            fw.dma("pool", x1s[tt * 128:(tt + 1) * 128, :], xt[lb][:], reads=[r_xt[lb]])
            fw.op("act", lambda e: e.activation(out=h2[b][:], in_=xt[lb][:], func=AF.Square,
                                                accum_out=st[b][:, 0:1]),
                  reads=[r_xt[lb]], writes=[r_h2[b], r_st[b]])
            fw.op("dve", lambda e: e.tensor_scalar(out=st[b][:, 1:2], in0=st[b][:, 0:1], scalar1=1.0 / D,
                                                   scalar2=EPS, op0=ALU.mult, op1=ALU.add),
                  reads=[r_st[b]], writes=[r_st[b]])
            fw.op("act", lambda e: e.activation(out=st[b][:, 2:3], in_=st[b][:, 1:2], func=AF.Sqrt),
                  reads=[r_st[b]], writes=[r_st[b]])
            fw.op("dve", lambda e: e.reciprocal(out=st[b][:, 3:4], in_=st[b][:, 2:3]),
                  reads=[r_st[b]], writes=[r_st[b]])
            fw.op("dve", lambda e: e.scalar_tensor_tensor(
                out=h2[b][:], in0=xt[lb][:], scalar=st[b][:, 3:4], in1=G[:, 1, :], op0=ALU.mult, op1=ALU.mult),
                reads=[r_xt[lb], r_st[b], r_G], writes=[r_h2[b]])
            for hb in range(2):
                pv = ps[pb + hb][:].bitcast(BF16)
                for kk in range(8):
                    k = hb * 8 + kk
                    fw.op("pe", lambda e, k=k, kk=kk, pv=pv: e.transpose(
                        pv[:, kk * 128:(kk + 1) * 128], h2[b][:, k * 128:(k + 1) * 128], ident[:]),
                        reads=[r_h2[b], r_ident], writes=[psr[pb + hb]], partial=(kk > 0))
                fw.op("dve" if hb == 0 else "act",
                      (lambda e, hb=hb, pv=pv: e.tensor_copy(
                          out=h2Tb[(tt // 4) % 2][:, hb * 8:(hb + 1) * 8, (tt % 4) * 128:(tt % 4 + 1) * 128],
                          in_=pv.rearrange("p (k t) -> p k t", k=8))) if hb == 0 else
                      (lambda e, hb=hb, pv=pv: e.activation(
                          out=h2Tb[(tt // 4) % 2][:, hb * 8:(hb + 1) * 8, (tt % 4) * 128:(tt % 4 + 1) * 128],
                          in_=pv.rearrange("p (k t) -> p k t", k=8), func=AF.Copy)),
                      reads=[psr[pb + hb]], writes=[r_h2Tb[(tt // 4) % 2]], partial=not (hb == 0 and tt % 4 == 0))
            if tt % 4 == 3:
                fw.dma("pool", h2s[tt // 4], h2Tb[(tt // 4) % 2][:].rearrange("p k t -> p (k t)"),
                       reads=[r_h2Tb[(tt // 4) % 2]])
        fw.barrier()

    with ExitStack() as e2:
        G2 = sb(e2, nc, "G2", [128, D], F32)
        r_G = Res()
        fw.dma("sp", G2[:], gains[2, :].partition_broadcast(128), writes=[r_G])
        h2Tl = [sb(e2, nc, "h2Tl%d" % i, [128, KD, 512], BF16) for i in range(2)]
        r_h2Tl = [Res() for _ in range(2)]

        def load_h2(tb):
            fw.dma("sp", h2Tl[tb % 2][:].rearrange("p k t -> p (k t)"), h2s[tb], writes=[r_h2Tl[tb % 2]])

        load_h2(0)
        FS = 512
        NFS = DFF // FS
        NWB = 2
        wgb = [sb(e2, nc, "wgb%d" % i, [128, KD, FS], BF16) for i in range(NWB)]
        wub = [sb(e2, nc, "wub%d" % i, [128, KD, FS], BF16) for i in range(NWB)]
        r_wg = [Res() for _ in range(NWB)]
        r_wu = [Res() for _ in range(NWB)]
        NDB = 4
        wdb = [sb(e2, nc, "wdb%d" % i, [128, D // 2], BF16) for i in range(NDB)]
        r_wd = [Res() for _ in range(NDB)]
        yh = [sb(e2, nc, "yh%d" % i, [128, D // 2], F32) for i in range(4)]
        r_yh = [Res() for _ in range(4)]
        aT = sb(e2, nc, "aT", [128, NF, 512], BF16)
        r_aT = [Res() for _ in range(NF)]
        sg = [sb(e2, nc, "sg%d" % i, [128, 512], F32) for i in range(2)]
        r_sg = [Res() for _ in range(2)]
        x1b = [sb(e2, nc, "x1b%d" % i, [128, D], F32) for i in range(2)]
        r_x1b = [Res() for _ in range(2)]
        tmp2 = [sb(e2, nc, "tmpb%d" % i, [128, 512], F32) for i in range(2)]
        r_tmp2 = [Res() for _ in range(2)]
        junk2 = sb(e2, nc, "junk2", [128, D // 2], BF16)
        r_junk2 = Res()
        st2 = [sb(e2, nc, "st2%d" % i, [128, 8], F32) for i in range(2)]
        r_st2 = [Res() for _ in range(2)]
        wg_v = wg.rearrange("(k p) c -> p k c", p=128)
        wu_v = wu.rearrange("(k p) c -> p k c", p=128)

        gu_seq = [(tb, s) for tb in range(NTB) for s in range(NFS)]

        def load_gu(i):
            tb, s = gu_seq[i]
            b = i % NWB
            for k in range(0, KD, 4):
                fw.dma("pool", wgb[b][:, k:k + 4, :], wg_v[:, k:k + 4, s * FS:(s + 1) * FS], writes=[r_wg[b]],
                       partial=(k > 0))
            for k in range(0, KD, 4):
                fw.dma("pool", wub[b][:, k:k + 4, :], wu_v[:, k:k + 4, s * FS:(s + 1) * FS], writes=[r_wu[b]],
                       partial=(k > 0))

        wd_cnt = [0]

        def load_wd(f, ch):
            b = wd_cnt[0] % NDB
            wd_cnt[0] += 1
            fw.dma("pool", wdb[b][:], wd[f * 128:(f + 1) * 128, ch * (D // 2):(ch + 1) * (D // 2)],
                   writes=[r_wd[b]])
            return b

        gi = 0
        load_gu(0)
        pcount = 0
        for tb in range(NTB):
            if tb + 1 < NTB:
                load_h2(tb + 1)
            hcur, r_hcur = h2Tl[tb % 2], r_h2Tl[tb % 2]
            for s in range(NFS):
                b = gi % NWB
                if gi + 1 < len(gu_seq):
                    load_gu(gi + 1)
                gi += 1
                for c in range(FS // 128):
                    f = s * (FS // 128) + c
                    pg = (pcount % 2) * 2
                    pcount += 1
                    for k in range(KD):
                        fw.op("pe", lambda e, k=k, c=c, pg=pg: e.matmul(
                            ps[pg][:], lhsT=wgb[b][:, k, c * 128:(c + 1) * 128], rhs=hcur[:, k, :],
                            start=(k == 0), stop=(k == KD - 1)),
                            reads=[r_wg[b], r_hcur], writes=[psr[pg]])
                    for k in range(KD):
                        fw.op("pe", lambda e, k=k, c=c, pg=pg: e.matmul(
                            ps[pg + 1][:], lhsT=wub[b][:, k, c * 128:(c + 1) * 128], rhs=hcur[:, k, :],
                            start=(k == 0), stop=(k == KD - 1)),
                            reads=[r_wu[b], r_hcur], writes=[psr[pg + 1]])
                    sb_ = f % 2
                    fw.op("act", lambda e, pg=pg, sb_=sb_: e.activation(out=sg[sb_][:], in_=ps[pg][:], func=AF.Silu),
                          reads=[psr[pg]], writes=[r_sg[sb_]])
                    fw.op("dve", lambda e, pg=pg, sb_=sb_, f=f: e.tensor_tensor(
                        out=aT[:, f, :], in0=sg[sb_][:], in1=ps[pg + 1][:], op=ALU.mult),
                        reads=[r_sg[sb_], psr[pg + 1]], writes=[r_aT[f]])
            for ch in range(2):
                wb_next = load_wd(0, ch)
                for f in range(NF):
                    wb = wb_next
                    if f + 1 < NF:
                        wb_next = load_wd(f + 1, ch)
                    for t4 in range(4):
                        for c2 in range(2):
                            bnk = t4 * 2 + c2
                            fw.op("pe", lambda e, f=f, t4=t4, c2=c2, wb=wb, bnk=bnk: e.matmul(
                                ps[bnk][:], lhsT=aT[:, f, t4 * 128:(t4 + 1) * 128],
                                rhs=wdb[wb][:, c2 * 512:(c2 + 1) * 512], start=(f == 0), stop=(f == NF - 1)),
                                reads=[r_aT[f], r_wd[wb]], writes=[psr[bnk]])
                if ch == 0:
                    for t4 in range(4):
                        for c2 in range(2):
                            bnk = t4 * 2 + c2
                            if c2 == 0:
                                fw.op("dve", lambda e: e.tensor_copy(out=yh[t4][:, 0:512], in_=ps[bnk][:]),
                                      reads=[psr[bnk]], writes=[r_yh[t4]])
                            else:
                                fw.op("act", lambda e: e.activation(out=yh[t4][:, 512:1024], in_=ps[bnk][:],
                                                                    func=AF.Copy),
                                      reads=[psr[bnk]], writes=[r_yh[t4]], partial=True)
                    continue
                for t4 in range(4):
                    tt = tb * 4 + t4
                    ob = tt % 2
                    fw.dma("sp", x1b[ob][:], x1s[tt * 128:(tt + 1) * 128, :], writes=[r_x1b[ob]])
                    fw.op("act", lambda e: e.activation(out=junk2[:], in_=yh[t4][:], func=AF.Square,
                                                        accum_out=st2[ob][:, 0:1]),
                          reads=[r_yh[t4]], writes=[r_junk2, r_st2[ob]], partial=True)
                    for c2 in range(2):
                        fw.op("act", lambda e, c2=c2: e.activation(
                            out=junk2[:, 0:512], in_=ps[t4 * 2 + c2][:], func=AF.Square,
                            accum_out=st2[ob][:, 1 + c2:2 + c2]),
                            reads=[psr[t4 * 2 + c2]], writes=[r_junk2, r_st2[ob]], partial=True)
                    fw.op("dve", lambda e: e.tensor_reduce(out=st2[ob][:, 4:5], in_=st2[ob][:, 0:3], axis=AX.X,
                                                           op=ALU.add),
                          reads=[r_st2[ob]], writes=[r_st2[ob]])
                    fw.op("dve", lambda e: e.tensor_scalar(out=st2[ob][:, 5:6], in0=st2[ob][:, 4:5], scalar1=1.0 / D,
                                                           scalar2=EPS, op0=ALU.mult, op1=ALU.add),
                          reads=[r_st2[ob]], writes=[r_st2[ob]])
                    fw.op("act", lambda e: e.activation(out=st2[ob][:, 6:7], in_=st2[ob][:, 5:6], func=AF.Sqrt),
                          reads=[r_st2[ob]], writes=[r_st2[ob]])
                    fw.op("dve", lambda e: e.reciprocal(out=st2[ob][:, 7:8], in_=st2[ob][:, 6:7]),
                          reads=[r_st2[ob]], writes=[r_st2[ob]])
                    for cg in range(4):
                        if cg < 2:
                            src, rsrc = yh[t4][:, cg * 512:(cg + 1) * 512], r_yh[t4]
                        else:
                            src, rsrc = ps[t4 * 2 + cg - 2][:], psr[t4 * 2 + cg - 2]
                        fw.op("dve", lambda e, cg=cg, src=src: e.scalar_tensor_tensor(
                            out=tmp2[cg % 2][:], in0=src, scalar=st2[ob][:, 7:8],
                            in1=G2[:, cg * 512:(cg + 1) * 512], op0=ALU.mult, op1=ALU.mult),
                            reads=[rsrc, r_st2[ob], r_G], writes=[r_tmp2[cg % 2]])
                        fw.op("dve", lambda e, cg=cg: e.tensor_tensor(
                            out=x1b[ob][:, cg * 512:(cg + 1) * 512], in0=tmp2[cg % 2][:],
                            in1=x1b[ob][:, cg * 512:(cg + 1) * 512], op=ALU.add),
                            reads=[r_tmp2[cg % 2], r_x1b[ob]], writes=[r_x1b[ob]])
                    fw.dma("sp", xout[tt * 128:(tt + 1) * 128, :], x1b[ob][:], reads=[r_x1b[ob]])
        fw.barrier()
    fw.finish()
    return nc


S_FULL = 16384
D_MODEL = 2048
N_CORES = 8


def _lambda_init(layer):
    return 0.8 - 0.6 * math.exp(-0.3 * layer)


_PROGS = {}


def _prog(name):
    if name not in _PROGS:
        _PROGS[name] = build_A(S_FULL) if name == "A" else build_B(S_FULL // N_CORES)
    return _PROGS[name]


def kernel(x, w_in, w_o, sb_out_norm, diff_subln, lambda_q1, lambda_k1, lambda_q2, lambda_k2,
           pre_mix_norm, post_mix_norm, pre_ffn_norm, post_ffn_norm, w_gate, w_up, w_down):
    f32 = lambda a: np.ascontiguousarray(np.asarray(a, dtype=np.float32))
    xc = f32(x)[0]
    S, D = xc.shape
    NT = S // N_CORES
    C, Sn = rope_tables(S)
    PT = rope_perm()
    cores = list(range(N_CORES))
    for l in range(2):
        li = _lambda_init(l)
        consts = np.tile(np.array([[li, 1.0 - li]], np.float32), (128, 1))
        lamv = f32(np.stack([lambda_q1[l], lambda_k1[l], lambda_q2[l], lambda_k2[l]]))
        w_in_l = f32(w_in[l])
        in_maps = []
        for c in cores:
            in_maps.append(dict(
                x=xc, wa=prep_wa(w_in_l, c), gpre=f32(pre_mix_norm[l]),
                hg=f32(np.stack([sb_out_norm[l][c], diff_subln[l][c]], axis=1)),
                lamv=lamv, consts=consts, cos_t=C, sin_t=Sn, permT=PT))
        res = run_bass_kernel_spmd(_prog("A"), in_maps, core_ids=cores)
        mix = [np.asarray(r["mixT"]) for r in res.results]
        mixT = np.concatenate([m[0:128] for m in mix] + [m[128:256] for m in mix], axis=0)
        gains = f32(np.stack([post_mix_norm[l], pre_ffn_norm[l], post_ffn_norm[l]]))
        wo_l, wg_l, wu_l, wd_l = f32(w_o[l]), f32(w_gate[l]), f32(w_up[l]), f32(w_down[l])
        in_maps = []
        for i in cores:
            mt = mixT[:, i * NT:(i + 1) * NT].reshape(D // 128, 128, NT // 128, 128).transpose(2, 1, 0, 3)
            in_maps.append(dict(mixt=np.ascontiguousarray(mt), x=np.ascontiguousarray(xc[i * NT:(i + 1) * NT]),
                                wo=wo_l, gains=gains, wg=wg_l, wu=wu_l, wd=wd_l))
        res = run_bass_kernel_spmd(_prog("B"), in_maps, core_ids=cores)
        xc = np.concatenate([np.asarray(r["xout"]) for r in res.results], axis=0)
    return xc[None].astype(np.float32)
```
